# Optimizing a Trainium2 kernel written in Bass

```python
import jax, jax.numpy as jnp
from jax import lax
import numpy as np

D_MODEL = 2048
BATCH = 2
SEQ = 16384
DEPTH = 1

ATTN_HEADS = 16
ATTN_KV_HEADS = 4
ATTN_HEAD_DIM = 64
ATTN_GROUP = ATTN_HEADS // ATTN_KV_HEADS
ATTN_WIDTH = ATTN_HEADS * ATTN_HEAD_DIM
KV_WIDTH = ATTN_KV_HEADS * ATTN_HEAD_DIM
WINDOW = 128
ATTN_BLOCK = 128
ROPE_THETA = 10000.0

HGRN_WIDTH = D_MODEL // 2
HGRN_EXPAND = 128
HGRN_HEADS = HGRN_WIDTH // HGRN_EXPAND
HGRN_KEY_DIM = HGRN_EXPAND
HGRN_VALUE_DIM = HGRN_WIDTH // HGRN_HEADS
HGRN_FORGET_WIDTH = HGRN_HEADS * HGRN_KEY_DIM
HGRN_CHUNK = 64

NORM_EPS = 1e-6

IN_SPLITS = (
    ATTN_WIDTH,
    KV_WIDTH,
    KV_WIDTH,
    ATTN_WIDTH,
    HGRN_FORGET_WIDTH,
    HGRN_FORGET_WIDTH,
    HGRN_WIDTH,
    HGRN_WIDTH,
    D_MODEL,
    D_MODEL,
)
IN_WIDTH = int(sum(IN_SPLITS))
SPLIT_POINTS = tuple(int(s) for s in np.cumsum(IN_SPLITS)[:-1])

kernel_name = "hybrid_swa_sink_hgrn2_gated_merge"


def rms_norm(x, gain):
    xf = x.astype(jnp.float32)
    y = xf * lax.rsqrt(jnp.mean(xf * xf, axis=-1, keepdims=True) + NORM_EPS)
    return (y * gain.astype(jnp.float32)).astype(x.dtype)


def rotary(x, positions):
    half = x.shape[-1] // 2
    inv_freq = ROPE_THETA ** (-jnp.arange(half, dtype=jnp.float32) / half)
    ang = positions.astype(jnp.float32)[..., None] * inv_freq
    cos = jnp.cos(ang)[:, :, None, :]
    sin = jnp.sin(ang)[:, :, None, :]
    xf = x.astype(jnp.float32)
    x1, x2 = xf[..., :half], xf[..., half:]
    return jnp.concatenate([x1 * cos - x2 * sin, x2 * cos + x1 * sin], axis=-1).astype(x.dtype)


def sliding_window_attention(q, k, v, sinks):
    B, T = q.shape[0], q.shape[1]
    nb = T // ATTN_BLOCK
    qb = q.reshape(B, nb, ATTN_BLOCK, ATTN_KV_HEADS, ATTN_GROUP, ATTN_HEAD_DIM)

    def with_prev(a):
        ab = a.reshape(B, nb, ATTN_BLOCK, ATTN_KV_HEADS, ATTN_HEAD_DIM)
        prev = jnp.pad(ab[:, :-1], ((0, 0), (1, 0), (0, 0), (0, 0), (0, 0)))
        return jnp.concatenate([prev, ab], axis=2)

    kb, vb = with_prev(k), with_prev(v)
    scores = jnp.einsum('bnqhgd,bnkhd->bnhgqk', qb, kb).astype(jnp.float32) * (ATTN_HEAD_DIM ** -0.5)
    q_pos = jnp.arange(ATTN_BLOCK)[:, None] + ATTN_BLOCK
    k_pos = jnp.arange(2 * ATTN_BLOCK)[None, :]
    rel = q_pos - k_pos
    band = (rel >= 0) & (rel < WINDOW)
    has_prev = (jnp.arange(nb) > 0)[:, None, None] | (k_pos >= ATTN_BLOCK)[None]
    mask = band[None] & has_prev
    scores = jnp.where(mask[None, :, None, None], scores, -jnp.inf)
    sink = jnp.broadcast_to(
        sinks.astype(jnp.float32).reshape(1, 1, ATTN_KV_HEADS, ATTN_GROUP, 1, 1),
        scores.shape[:-1] + (1,))
    probs = jax.nn.softmax(jnp.concatenate([scores, sink], axis=-1), axis=-1)[..., :-1]
    out = jnp.einsum('bnhgqk,bnkhd->bnqhgd', probs.astype(v.dtype), vb)
    return out.reshape(B, T, ATTN_WIDTH)


def hgrn2_chunked(q, k, v, log_f):
    B, T, H, dk = q.shape
    dv = v.shape[-1]
    nc = T // HGRN_CHUNK

    def to_chunks(a):
        return a.reshape(B, nc, HGRN_CHUNK, H, a.shape[-1]).transpose(1, 0, 2, 3, 4)

    qc, kc, vc = to_chunks(q), to_chunks(k), to_chunks(v)
    bc = jnp.cumsum(to_chunks(log_f), axis=2)
    causal = jnp.tril(jnp.ones((HGRN_CHUNK, HGRN_CHUNK), dtype=bool))[None, :, :, None, None]

    def step(S, inp):
        qt, kt, vt, bt = inp
        diff = bt[:, :, None] - bt[:, None, :]
        decay = jnp.exp(jnp.where(causal, diff, -jnp.inf))
        scores = jnp.einsum('bthd,bshd,btshd->bhts', qt, kt, decay)
        o_intra = jnp.einsum('bhts,bshv->bthv', scores, vt)
        o_inter = jnp.einsum('bthd,bhdv->bthv', qt * jnp.exp(bt), S)
        b_last = bt[:, -1]
        k_dec = kt * jnp.exp(b_last[:, None] - bt)
        S_new = jnp.exp(b_last)[..., None] * S + jnp.einsum('bshd,bshv->bhdv', k_dec, vt)
        return S_new, o_intra + o_inter

    S0 = jnp.zeros((B, H, dk, dv), dtype=jnp.float32)
    _, o = lax.scan(step, S0, (qc, kc, vc, bc))
    return o.transpose(1, 0, 2, 3, 4).reshape(B, T, H, dv)


def setup_inputs(seed: int = 0) -> dict:
    key = jax.random.key(seed)
    ks = jax.random.split(key, 11)
    f32 = jnp.float32
    x = jax.random.normal(ks[0], (BATCH, SEQ, D_MODEL), f32)
    positions = jnp.broadcast_to(jnp.arange(SEQ, dtype=jnp.int32), (BATCH, SEQ))
    norm_gain = 1.0 + 0.02 * jax.random.normal(ks[1], (DEPTH, D_MODEL), f32)
    w_in = jax.random.normal(ks[2], (DEPTH, D_MODEL, IN_WIDTH), f32) * D_MODEL ** -0.5
    attn_sinks = 0.5 * jax.random.normal(ks[3], (DEPTH, ATTN_HEADS), f32)
    hgrn_lower_bounds = 0.1 * jax.random.normal(ks[4], (DEPTH + 1, HGRN_FORGET_WIDTH), f32)
    hgrn_norm_gain = 1.0 + 0.02 * jax.random.normal(ks[5], (DEPTH, HGRN_HEADS, HGRN_VALUE_DIM), f32)
    w_attn_out = jax.random.normal(ks[6], (DEPTH, ATTN_WIDTH, D_MODEL), f32) * ATTN_WIDTH ** -0.5
    w_hgrn_out = jax.random.normal(ks[7], (DEPTH, HGRN_WIDTH, D_MODEL), f32) * HGRN_WIDTH ** -0.5
    w_o = jax.random.normal(ks[8], (DEPTH, D_MODEL, D_MODEL), f32) * D_MODEL ** -0.5
    final_norm_gain = 1.0 + 0.02 * jax.random.normal(ks[9], (D_MODEL,), f32)
    return {"x": x, "positions": positions, "norm_gain": norm_gain, "w_in": w_in,
            "attn_sinks": attn_sinks, "hgrn_lower_bounds": hgrn_lower_bounds,
            "hgrn_norm_gain": hgrn_norm_gain, "w_attn_out": w_attn_out,
            "w_hgrn_out": w_hgrn_out, "w_o": w_o, "final_norm_gain": final_norm_gain}


def reference(x, positions, norm_gain, w_in, attn_sinks, hgrn_lower_bounds, hgrn_norm_gain,
              w_attn_out, w_hgrn_out, w_o, final_norm_gain):
    B, T = x.shape[0], x.shape[1]
    lower_bounds = jnp.cumsum(jax.nn.softmax(hgrn_lower_bounds.astype(jnp.float32), axis=0), axis=0)
    for l in range(DEPTH):
        h = rms_norm(x, norm_gain[l])
        proj = h @ w_in[l]
        (aq, ak, av, a_gate, hq, hf, hi, h_gate, m_attn, m_hgrn) = jnp.split(proj, SPLIT_POINTS, axis=-1)

        aq = rotary(aq.reshape(B, T, ATTN_HEADS, ATTN_HEAD_DIM), positions)
        ak = rotary(ak.reshape(B, T, ATTN_KV_HEADS, ATTN_HEAD_DIM), positions)
        av = av.reshape(B, T, ATTN_KV_HEADS, ATTN_HEAD_DIM)
        attn = sliding_window_attention(aq, ak, av, attn_sinks[l])
        y_attn = (attn * jax.nn.silu(a_gate)) @ w_attn_out[l]

        lb = lower_bounds[l]
        hf32 = hf.astype(jnp.float32)
        forget = lb + (1.0 - lb) * jax.nn.sigmoid(hf32)
        log_f = jnp.log(forget)
        k_in = (1.0 - lb) * jax.nn.sigmoid(-hf32)
        q_r = jax.nn.silu(hq.astype(jnp.float32)) * (HGRN_KEY_DIM ** -0.5)
        heads_k = (B, T, HGRN_HEADS, HGRN_KEY_DIM)
        o = hgrn2_chunked(q_r.reshape(heads_k), k_in.reshape(heads_k),
                          hi.astype(jnp.float32).reshape(B, T, HGRN_HEADS, HGRN_VALUE_DIM),
                          log_f.reshape(heads_k))
        o = rms_norm(o, hgrn_norm_gain[l]).reshape(B, T, HGRN_WIDTH).astype(x.dtype)
        y_hgrn = (o * jax.nn.silu(h_gate)) @ w_hgrn_out[l]

        merged = jax.nn.sigmoid(m_attn) * y_attn + jax.nn.sigmoid(m_hgrn) * y_hgrn
        x = x + merged @ w_o[l]
    return rms_norm(x, final_norm_gain)
```

```python
import os
import numpy as np
import ml_dtypes
from contextlib import ExitStack
import concourse.bass as bass
import concourse.mybir as mybir
from concourse.bass_utils import run_bass_kernel_spmd

F32 = mybir.dt.float32
BF16 = mybir.dt.bfloat16
I32 = mybir.dt.int32
AF = mybir.ActivationFunctionType
ALU = mybir.AluOpType
AX = mybir.AxisListType

SAME_ENG_SYNC = True

D = 2048
SEQ = 16384
NCORE = 8
SEG = 4096
NT = 256
NB = NT // 128
NCH = NT // 64
NTILES = SEG // NT
WARM = NT
KC = 16
EPS = 1e-6
TWO_PI = 6.2831845

OFF_AQ, OFF_AK, OFF_AV, OFF_AG = 0, 1024, 1280, 1536
OFF_HQ, OFF_HF, OFF_HI, OFF_HG = 2560, 3584, 4608, 5632
OFF_MA, OFF_MH = 6656, 8704
IN_W = 10752

G_K, G_V = 0, 1
G_QA = 2
G_H = 6
G_M = 14
G_O = 30
NGRP = 34
SLOT = 8192
NSLOT = 3
EPERM = (0, 2, 1, 3)


class Res:
    __slots__ = ("name", "writer", "readers", "const", "excl")

    def __init__(self, name, const=False, excl=False):
        self.name = name
        self.writer = None
        self.readers = []
        self.const = const
        self.excl = excl


class Op:
    __slots__ = ("eng", "fn", "deps", "marked", "event", "chan")

    def __init__(self, eng, fn, chan=None):
        self.eng = eng
        self.fn = fn
        self.deps = []
        self.marked = False
        self.event = None
        self.chan = chan


class Sched:
    ENG = ("pe", "act", "dve", "pool", "sp")

    def __init__(self, nc, stack):
        self.nc = nc
        self.stack = stack
        self.ops = {e: [] for e in self.ENG}
        self.sems = {}
        self.chan_cnt = {}
        self.n_wait = 0

    def sem(self, key):
        if key not in self.sems:
            name = "s_" + "_".join(str(k) for k in key)
            self.sems[key] = self.stack.enter_context(self.nc.semaphore(name))
        return self.sems[key]

    def add(self, eng, fn, reads=(), writes=(), chan=None):
        op = Op(eng, fn, chan)
        if any(r.excl for r in reads):
            writes = list(writes) + [r for r in reads if r.excl]
            reads = [r for r in reads if not r.excl]
        deps = {}
        for r in reads:
            if r.writer is not None:
                deps[id(r.writer)] = r.writer
        for w in writes:
            if w.writer is not None:
                deps[id(w.writer)] = w.writer
            for rd in w.readers:
                deps[id(rd)] = rd
        for r in reads:
            if not r.const:
                r.readers.append(op)
        for w in writes:
            w.writer = op
            w.readers = []
        dl = []
        for d in deps.values():
            if d is op:
                continue
            if d.eng == eng and d.chan is None and chan is None:
                if eng == "pe" or not SAME_ENG_SYNC:
                    continue
            dl.append(d)
        op.deps = dl
        if chan is not None:
            n = self.chan_cnt.get(chan, 0) + 16
            self.chan_cnt[chan] = n
            op.event = (("D", chan), n)
            op.marked = True
        self.ops[eng].append(op)
        return op

    def prepare(self):
        for e in self.ENG:
            for op in self.ops[e]:
                for d in op.deps:
                    d.marked = True
        for e in self.ENG:
            c = 0
            for op in self.ops[e]:
                if op.chan is None and op.marked:
                    c += 1
                    op.event = (("E", e), c)
        for e in self.ENG:
            for op in self.ops[e]:
                if op.event is not None:
                    self.sem(op.event[0])

    def emit_one(self, e, eng):
        seen = {}
        for op in self.ops[e]:
            waits = {}
            for d in op.deps:
                k, v = d.event
                if waits.get(k, 0) < v:
                    waits[k] = v
            for k, v in waits.items():
                if seen.get(k, 0) >= v:
                    continue
                seen[k] = v
                eng.wait_ge(self.sem(k), v)
                self.n_wait += 1
            inst = op.fn(eng)
            if op.chan is not None:
                inst.then_inc(self.sem(op.event[0]), 16)
            elif op.marked:
                inst.then_inc(self.sem(op.event[0]), 1)


def build_program(NTILES=NTILES):
    SEG = NTILES * NT
    nc = bass.Bass("TRN2", target_bir_lowering=False)

    def din(name, shape, dt):
        return nc.dram_tensor(name, shape, dt, kind="ExternalInput").ap()

    x_d = din("x", [WARM + SEG, D], F32)
    pos_d = din("pos", [1, WARM + SEG], I32)
    w_in_d = din("w_in", [D, IN_W], F32)
    w_ao_d = din("w_ao", [1024, D], F32)
    w_ho_d = din("w_ho", [1024, D], F32)
    w_o_d = din("w_o", [D, D], F32)
    ident_d = din("ident", [128, 128], BF16)
    onesd_d = din("onesd", [128, 128], BF16)
    prot_d = din("prot", [128, 128], BF16)
    maskA_d = din("maskA", [128, 512], F32)
    maskF_d = din("maskF", [128, 512], F32)
    maskH_d = din("maskH", [128, 128], F32)
    rmask_d = din("rmask", [128, NT], F32)
    invf_d = din("invf", [128, 1], F32)
    gin_d = din("gin", [128, KC], F32)
    fgain_d = din("fgain", [1, D], F32)
    sink_d = din("sink", [1, 16], F32)
    lbraw_d = din("lbraw", [128, 2, 8], F32)
    hgain_d = din("hgain", [128, 8], F32)
    out_d = nc.dram_tensor("out", [SEG, D], F32, kind="ExternalOutput").ap()
    wsc_d = nc.dram_tensor("wsc", [NGRP, 128, SLOT], BF16, kind="Internal").ap()

    with ExitStack() as st:
        S = Sched(nc, st)
        A = S.add

        def sb(name, shape, dt):
            return st.enter_context(nc.sbuf_tensor("sb_" + name, shape, dt))

        def ps(name, shape, dt):
            return st.enter_context(nc.psum_tensor("ps_" + name, shape, dt))

        ident = sb("ident", [128, 128], BF16)
        onesd = sb("onesd", [128, 128], BF16)
        prot = sb("prot", [128, 128], BF16)
        maskA = sb("maskA", [128, 512], F32)
        maskF = sb("maskF", [128, 512], F32)
        maskH = sb("maskH", [128, 128], F32)
        rmask = sb("rmask", [128, NT], F32)
        invf = sb("invf", [128, 1], F32)
        gin = sb("gin", [128, KC], F32)
        fgain = sb("fgain", [128, D], F32)
        sinkb = sb("sinkb", [128, 16], F32)
        esk = sb("esk", [128, 16], F32)
        lbraw = sb("lbraw", [128, 2, 8], F32)
        lbd = sb("lbd", [128, 8], F32)
        lb = sb("lb", [128, 8], F32)
        oml = sb("oml", [128, 8], F32)
        hgain = sb("hgain", [128, 8], F32)
        r_const = Res("const", const=True)
        cdma = [(ident, ident_d), (onesd, onesd_d), (prot, prot_d), (maskA, maskA_d), (maskF, maskF_d),
                (maskH, maskH_d), (rmask, rmask_d), (invf, invf_d), (gin, gin_d), (lbraw, lbraw_d),
                (hgain, hgain_d)]
        for t_, d_ in cdma:
            A("sp", lambda e, t_=t_, d_=d_: e.dma_start(out=t_[:], in_=d_), writes=[r_const], chan="const")
        A("sp", lambda e: e.dma_start(out=fgain[:], in_=fgain_d[0:1, :].to_broadcast([128, D])),
          writes=[r_const], chan="const")
        A("sp", lambda e: e.dma_start(out=sinkb[:], in_=sink_d[0:1, :].to_broadcast([128, 16])),
          writes=[r_const], chan="const")
        r_c2 = Res("const2")
        A("dve", lambda e: e.tensor_tensor(out=lbd[:], in0=lbraw[:, 0, :], in1=lbraw[:, 1, :], op=ALU.subtract),
          reads=[r_const], writes=[r_c2])
        A("act", lambda e: e.activation(out=lb[:], in_=lbd[:], func=AF.Sigmoid), reads=[r_c2], writes=[r_c2])
        A("act", lambda e: e.activation(out=oml[:], in_=lbd[:], func=AF.Sigmoid, scale=-1.0), reads=[r_c2], writes=[r_c2])
        A("act", lambda e: e.activation(out=esk[:], in_=sinkb[:], func=AF.Exp), reads=[r_const], writes=[r_c2])
        r_c2.const = True
        CONST = [r_const, r_c2]

        PB = [ps("pb%d" % i, [128, 512], F32) for i in range(6)]
        r_PB = [Res("pb%d" % i, excl=True) for i in range(6)]
        PT = [ps("pt%d" % i, [128, 1024], BF16) for i in range(2)]
        r_PT = [Res("pt%d" % i, excl=True) for i in range(2)]
        cnt = {"pb": 0, "pt": 0, "ws": 0, "ev": 0}

        def next_pb(avoid=()):
            while True:
                i = cnt["pb"] % 6
                cnt["pb"] += 1
                if not any(r_PB[i] is r for r in avoid):
                    return PB[i], r_PB[i]

        def next_pt():
            i = cnt["pt"] % 2
            cnt["pt"] += 1
            return PT[i], r_PT[i]

        r_wsc = [Res("wsc%d" % g) for g in range(NGRP)]

        def cast(g, dst_lo, ncols, src, col0, nk):
            dst = wsc_d[g][:, dst_lo:dst_lo + nk * ncols].rearrange("p (k c) -> p k c", k=nk)
            s_ = src[:, col0:col0 + ncols].rearrange("(k p) c -> p k c", p=128)
            A("pool", lambda e: e.dma_start(out=dst, in_=s_), writes=[r_wsc[g]], chan="wsc%d" % g)

        def cast_sub(g, width, sub_lo, ncols, src, col0, nk):
            dst = wsc_d[g][:, 0:nk * width].rearrange("p (k c) -> p k c", k=nk)[:, :, sub_lo:sub_lo + ncols]
            s_ = src[:, col0:col0 + ncols].rearrange("(k p) c -> p k c", p=128)
            A("pool", lambda e: e.dma_start(out=dst, in_=s_), writes=[r_wsc[g]], chan="wsc%d" % g)

        for g in range(4):
            for dup in range(2):
                cast_sub(G_K, 512, g * 128 + dup * 64, 64, w_in_d, OFF_AK + g * 64, KC)
        cast_sub(G_V, 256, 0, 256, w_in_d, OFF_AV, KC)
        for g in range(4):
            cast_sub(G_QA + g, 512, 0, 256, w_in_d, OFF_AQ + g * 256, KC)
            cast_sub(G_QA + g, 512, 256, 256, w_in_d, OFF_AG + g * 256, KC)
        for hd in range(8):
            for q, off in enumerate((OFF_HF, OFF_HQ, OFF_HI, OFF_HG)):
                cast_sub(G_H + hd, 512, q * 128, 128, w_in_d, off + hd * 128, KC)
        for dc in range(16):
            cast(G_M + dc, 0, 128, w_in_d, OFF_MA + dc * 128, KC)
            cast(G_M + dc, 2048, 128, w_in_d, OFF_MH + dc * 128, KC)
            cast(G_M + dc, 4096, 128, w_ao_d, dc * 128, 8)
            cast(G_M + dc, 5120, 128, w_ho_d, dc * 128, 8)
        for cg in range(4):
            cast_sub(G_O + cg, 512, 0, 512, w_o_d, cg * 512, KC)

        WS = [sb("ws%d" % i, [128, SLOT], BF16) for i in range(NSLOT)]
        r_WS = [Res("ws%d" % i) for i in range(NSLOT)]

        def get_w(g, nelem=SLOT):
            i = cnt["ws"] % NSLOT
            cnt["ws"] += 1
            A("sp", lambda e: e.dma_start(out=WS[i][:, 0:nelem], in_=wsc_d[g][:, 0:nelem]),
              reads=[r_wsc[g]], writes=[r_WS[i]], chan="ws%d" % i)
            return WS[i], r_WS[i]

        xs = sb("xs", [128, D], F32); r_xs = Res("xs")
        hn = [sb("hn%d" % i, [128, D], BF16) for i in range(NB)]
        r_hn = [Res("hn%d" % i) for i in range(NB)]
        st_small = [sb("stt%d" % i, [128, 4], F32) for i in range(NB)]
        r_sts = [Res("stt%d" % i) for i in range(NB)]
        hT = sb("hT", [128, KC, NT], BF16); r_hT = Res("hT")
        posi = sb("posi", [128, NT], I32); r_posi = Res("posi")
        rp = [sb("rp%d" % i, [128, NT], F32) for i in range(4)]
        r_rp = [Res("rp%d" % i) for i in range(4)]
        cosT = sb("cosT", [128, NT], F32); sinT = sb("sinT", [128, NT], F32)
        r_rope = Res("rope")
        KT = [sb("KT%d" % g, [128, 128 + NT], BF16) for g in range(4)]
        r_KT = [Res("KT%d" % g) for g in range(4)]
        Vt = sb("Vt", [128, NB + 1, 256], BF16); r_Vt = Res("Vt")
        QT = [sb("QT%d" % i, [128, 2, NT], BF16) for i in range(2)]
        r_QT = [Res("QT%d" % i) for i in range(2)]
        AG = [sb("AG%d" % i, [128, 2, NT], BF16) for i in range(2)]
        r_AG = [Res("AG%d" % i) for i in range(2)]
        GA = sb("GA", [128, 8, NT], BF16); r_GA = Res("GA")
        GH = sb("GH", [128, 8, NT], BF16); r_GH = Res("GH")
        HGs = sb("HGs", [128, 8, NT], BF16); r_HGs = [Res("HGs%d" % h) for h in range(8)]
        MG = sb("MG", [128, KC, NT], BF16); r_MG = Res("MG")
        qraw = [sb("qraw%d" % i, [128, NT], BF16) for i in range(2)]
        r_qraw = [Res("qraw%d" % i) for i in range(2)]
        rt1 = [sb("rt1_%d" % i, [128, NT], F32) for i in range(2)]
        rt2 = [sb("rt2_%d" % i, [128, NT], F32) for i in range(2)]
        r_rt = [Res("rt%d" % i) for i in range(2)]
        Sm = [sb("Sm0", [128, 1024], F32)] * 2
        r_Sm = [Res("Sm0")] * 2
        Pm = [sb("Pm%d" % i, [128, 1024], BF16) for i in range(2)]
        r_Pm = [Res("Pm%d" % i) for i in range(2)]
        PTs = [sb("PTs%d" % i, [128, 1024], BF16) for i in range(2)]
        r_PTs = [Res("PTs%d" % i) for i in range(2)]
        ast = [sb("ast%d" % i, [128, 24], F32) for i in range(2)]
        r_ast = [Res("ast%d" % i) for i in range(2)]
        Sf = [sb("Sf%d" % h, [128, 128], F32) for h in range(8)]
        r_Sf = [Res("Sf%d" % h) for h in range(8)]
        Sb = [[sb("Sb%d_%d" % (h, p), [128, 128], BF16) for p in range(2)] for h in range(8)]
        r_Sb = [[Res("Sb%d_%d" % (h, p)) for p in range(2)] for h in range(8)]
        QH = [sb("QH%d" % h, [128, NT], BF16) for h in range(8)]
        VH = [sb("VH%d" % h, [128, NB, 128], BF16) for h in range(8)]
        KTok = [sb("KTok%d" % h, [128, NB, 128], BF16) for h in range(8)]
        ATm = [sb("ATm%d" % h, [128, NT], BF16) for h in range(8)]
        EB = [sb("EB%d" % h, [128, NCH], F32) for h in range(8)]
        r_hd = [Res("hd%d" % h) for h in range(8)]
        HT = [[sb("HT0_%d" % k, [128, NT], F32) for k in range(5)]] * 2
        r_HT = [[Res("HT0_%d" % k) for k in range(5)]] * 2
        KH = [sb("KH%d" % i, [128, NT], BF16) for i in range(2)]
        r_KH = [Res("KH%d" % i) for i in range(2)]
        Tst = [sb("Tst%d" % h, [128, 128], F32) for h in range(4)] * 2
        r_Tst = [Res("Tst%d" % h) for h in range(4)] * 2
        SQ = [sb("SQ0", [128, 512], BF16)] * 2
        r_SQ = [Res("SQ0")] * 2
        RR = [sb("RR0", [128, 512], F32)] * 2
        r_RR = [Res("RR0")] * 2
        OT = [sb("OT0", [128, 512], F32)] * 2
        r_OT = [Res("OT0")] * 2
        if os.environ.get("DBG_TAPS"):
            ORAW = sb("ORAW", [128, 512], F32); r_ORAW = Res("ORAW")
        mt = [[sb("mt0_%d" % k, [128, NT], F32) for k in range(4)]] * 2
        r_mt = [[Res("mt0_%d" % k) for k in range(4)]] * 2
        fin = [sb("fin%d" % i, [128, D], F32) for i in range(NB)]
        r_fin = [Res("fin%d" % i) for i in range(NB)]
        junk = sb("junk", [128, 512], BF16); r_junk = Res("junk")
        fst = [sb("fst%d" % i, [128, 8], F32) for i in range(NB)]
        r_fst = [Res("fst%d" % i) for i in range(NB)]

        for h in range(8):
            A("pool", lambda e, h=h: e.memset(Sf[h][:], 0.0), writes=[r_Sf[h]])
            A("pool", lambda e, h=h: e.memset(Sb[h][0][:], 0.0), writes=[r_Sb[h][0]])
        for g in range(4):
            A("pool", lambda e, g=g: e.memset(KT[g][:], 0.0), writes=[r_KT[g]])
        A("pool", lambda e: e.memset(Vt[:], 0.0), writes=[r_Vt])

        def evac_engine():
            cnt["ev"] += 1
            return "act" if cnt["ev"] % 2 == 0 else "dve"

        def copy_op(eng, out, in_, reads, writes):
            if eng == "act":
                A("act", lambda e: e.activation(out=out, in_=in_, func=AF.Copy), reads=reads, writes=writes)
            else:
                A(eng, lambda e: e.tensor_copy(out=out, in_=in_), reads=reads, writes=writes)

        def prologue_p1(ti):
            row0 = (ti + 1) * NT
            for blk in range(NB):
                r0 = row0 + blk * 128
                A("sp", lambda e, r0=r0: e.dma_start(out=xs[:], in_=x_d[r0:r0 + 128, :]), writes=[r_xs], chan="xs")
                stt = st_small[blk]
                A("act", lambda e, blk=blk, stt=stt: e.activation(out=hn[blk][:], in_=xs[:], func=AF.Square,
                                                                accum_out=stt[:, 0:1]),
                  reads=[r_xs], writes=[r_hn[blk], r_sts[blk]])
                A("act", lambda e, stt=stt: e.activation(out=stt[:, 1:2], in_=stt[:, 0:1], func=AF.Ln,
                                                         scale=1.0 / D, bias=EPS),
                  reads=[r_sts[blk]], writes=[r_sts[blk]])
                A("act", lambda e, stt=stt: e.activation(out=stt[:, 2:3], in_=stt[:, 1:2], func=AF.Exp, scale=-0.5),
                  reads=[r_sts[blk]], writes=[r_sts[blk]])
                A("dve", lambda e, blk=blk, stt=stt: e.tensor_scalar(out=hn[blk][:], in0=xs[:], scalar1=stt[:, 2:3],
                                                                  scalar2=None, op0=ALU.mult),
                  reads=[r_xs, r_sts[blk]], writes=[r_hn[blk]])
            A("sp", lambda e: e.dma_start(out=posi[:], in_=pos_d[0:1, row0:row0 + NT].to_broadcast([128, NT])),
              writes=[r_posi], chan="posi")
            v, kf, tt, uc = rp[0], rp[1], rp[2], rp[3]
            RP = [r_posi] + r_rp
            P = "dve"
            A(P, lambda e: e.tensor_copy(out=v[:], in_=posi[:]), reads=RP + [r_rope], writes=RP)
            A(P, lambda e: e.tensor_scalar(out=v[:], in0=v[:], scalar1=invf[:, 0:1], scalar2=1.0, op0=ALU.mult, op1=ALU.mult),
              reads=RP + CONST, writes=RP)
            A(P, lambda e: e.tensor_copy(out=posi[:], in_=v[:]), reads=RP, writes=RP)
            A(P, lambda e: e.tensor_copy(out=kf[:], in_=posi[:]), reads=RP, writes=RP)
            A(P, lambda e: e.tensor_tensor(out=v[:], in0=v[:], in1=kf[:], op=ALU.subtract), reads=RP, writes=RP)
            A(P, lambda e: e.tensor_scalar(out=tt[:], in0=v[:], scalar1=0.5, scalar2=1.0, op0=ALU.is_gt, op1=ALU.mult), reads=RP, writes=RP)
            A(P, lambda e: e.tensor_tensor(out=v[:], in0=v[:], in1=tt[:], op=ALU.subtract), reads=RP, writes=RP)
            A(P, lambda e: e.tensor_scalar(out=tt[:], in0=v[:], scalar1=-0.5, scalar2=1.0, op0=ALU.is_lt, op1=ALU.mult), reads=RP, writes=RP)
            A(P, lambda e: e.tensor_tensor(out=v[:], in0=v[:], in1=tt[:], op=ALU.add), reads=RP, writes=RP)
            A(P, lambda e: e.tensor_scalar(out=uc[:], in0=v[:], scalar1=0.25, scalar2=None, op0=ALU.add), reads=RP, writes=RP)
            A(P, lambda e: e.tensor_scalar(out=tt[:], in0=uc[:], scalar1=0.5, scalar2=1.0, op0=ALU.is_gt, op1=ALU.mult), reads=RP, writes=RP)
            A(P, lambda e: e.tensor_tensor(out=uc[:], in0=uc[:], in1=tt[:], op=ALU.subtract), reads=RP, writes=RP)
            A("act", lambda e: e.activation(out=sinT[:], in_=v[:], func=AF.Sin, scale=TWO_PI), reads=RP, writes=[r_rope])
            A("act", lambda e: e.activation(out=cosT[:], in_=uc[:], func=AF.Sin, scale=TWO_PI), reads=RP, writes=[r_rope])

        def prologue_p2():
            for blk in range(NB):
                for half in range(2):
                    pt, r_pt = next_pt()
                    for k in range(8):
                        kc = half * 8 + k
                        A("pe", lambda e, pt=pt, k=k, kc=kc, blk=blk: e.transpose(
                            out=pt[:, k * 128:(k + 1) * 128], in_=hn[blk][:, kc * 128:(kc + 1) * 128], identity=ident[:]),
                          reads=[r_hn[blk]] + CONST, writes=[r_pt])
                    src = pt[:, :].rearrange("p (k c) -> p k c", k=8)
                    dst = hT[:, half * 8:half * 8 + 8, blk * 128:(blk + 1) * 128]
                    gb = gin[:, half * 8:half * 8 + 8].unsqueeze(2).to_broadcast([128, 8, 128])
                    A("dve", lambda e, src=src, dst=dst, gb=gb: e.tensor_tensor(out=dst, in0=src, in1=gb, op=ALU.mult),
                      reads=[r_pt] + CONST, writes=[r_hT])

        def proj_fm(bank, r_bank, w, r_w, wcol0, wstride, ncols_out=NT):
            for kc in range(KC):
                A("pe", lambda e, kc=kc: e.matmul(bank[:, 0:NT], lhsT=w[:, kc * wstride + wcol0: kc * wstride + wcol0 + 128],
                                                  rhs=hT[:, kc, :], start=(kc == 0), stop=(kc == KC - 1)),
                  reads=[r_w, r_hT], writes=[r_bank])

        pending = []

        def flush_pending():
            while pending:
                pending.pop(0)()

        def rope_evac(bank, r_bank, dst, r_dst):
            i = cnt.setdefault("rope", 0) % 2
            cnt["rope"] += 1
            A("act", lambda e: e.activation(out=qraw[i][:], in_=bank[:, 0:NT], func=AF.Copy),
              reads=[r_bank], writes=[r_qraw[i]])
            A("dve", lambda e: e.tensor_tensor(out=rt1[i][:], in0=bank[:, 0:NT], in1=cosT[:], op=ALU.mult),
              reads=[r_bank, r_rope], writes=[r_rt[i]])

            def second():
                b2, r_b2 = next_pb()
                A("pe", lambda e: e.matmul(b2[:, 0:NT], lhsT=prot[:], rhs=qraw[i][:], start=True, stop=True),
                  reads=[r_qraw[i]] + CONST, writes=[r_b2])
                A("dve", lambda e: e.tensor_tensor(out=rt2[i][:], in0=b2[:, 0:NT], in1=sinT[:], op=ALU.mult),
                  reads=[r_b2, r_rope, r_rt[i]], writes=[r_rt[i]])
                A(os.environ.get("DBG_ROPE_ENG", "pool"), lambda e: e.tensor_tensor(out=dst, in0=rt1[i][:], in1=rt2[i][:], op=ALU.add),
                  reads=[r_rt[i]], writes=[r_dst])
            pending.append(second)

        def attn_kv(ti):

            SUB = int(os.environ.get("DBG_SUB", "9"))
            wK, r_wK = get_w(G_K)
            for g in range(4):
                bank, r_bank = next_pb()
                proj_fm(bank, r_bank, wK, r_wK, g * 128, 512)
                flush_pending()
                if SUB == 1:
                    copy_op("act", KT[g][:, 128:128 + NT], bank[:, 0:NT], [r_bank], [r_KT[g]])
                    continue
                rope_evac(bank, r_bank, KT[g][:, 128:128 + NT], r_KT[g])
            if SUB <= 2:
                flush_pending()
                return
            wV, r_wV = get_w(G_V, KC * 256)
            for blk in range(NB):
                bank, r_bank = next_pb()
                for kc in range(KC):
                    A("pe", lambda e, kc=kc, blk=blk, bank=bank: e.matmul(
                        bank[:, 0:256], lhsT=hT[:, kc, blk * 128:(blk + 1) * 128], rhs=wV[:, kc * 256:(kc + 1) * 256],
                        start=(kc == 0), stop=(kc == KC - 1)), reads=[r_wV, r_hT], writes=[r_bank])
                flush_pending()
                copy_op(evac_engine(), Vt[:, blk + 1, :], bank[:, 0:256], [r_bank], [r_Vt])
            flush_pending()

        def attn_halo():
            for g in range(4):
                A("pool", lambda e, g=g: e.tensor_copy(out=KT[g][:, 0:128], in_=KT[g][:, NT:NT + 128]),
                  reads=[r_KT[g]], writes=[r_KT[g]])
            A("pool", lambda e: e.tensor_copy(out=Vt[:, 0, :], in_=Vt[:, NB, :]), reads=[r_Vt], writes=[r_Vt])

        def attn_proj(g):
            s = g % 2
            wQ, r_wQ = get_w(G_QA + g)
            for c in range(2):
                bank, r_bank = next_pb()
                proj_fm(bank, r_bank, wQ, r_wQ, c * 128, 512)
                flush_pending()
                rope_evac(bank, r_bank, QT[s][:, c, :], r_QT[s])
            for c in range(2):
                bank, r_bank = next_pb()
                proj_fm(bank, r_bank, wQ, r_wQ, 256 + c * 128, 512)
                flush_pending()
                A("act", lambda e, c=c, bank=bank: e.activation(out=AG[s][:, c, :], in_=bank[:, 0:NT], func=AF.Silu),
                  reads=[r_bank], writes=[r_AG[s]])
            flush_pending()

        def attn_core(g, ti):
            s = g % 2
            stA = {}

            def stage_A(j):
                a = (cnt.setdefault("att", 0)) % 2
                cnt["att"] += 1
                stA[j] = a
                bE, r_bE = next_pb()
                bO, r_bO = next_pb()
                for i in range(4):
                    c, half = i // 2, i % 2
                    bank, r_bank = (bE, r_bE) if half == 0 else (bO, r_bO)
                    lo = 64 * half
                    A("pe", lambda e, c=c, lo=lo, bank=bank: e.matmul(
                        bank[:, c * 256:(c + 1) * 256], lhsT=QT[s][lo:lo + 64, c, j * 128:(j + 1) * 128],
                        rhs=KT[g][lo:lo + 64, j * 128:j * 128 + 256], start=True, stop=True),
                      reads=[r_QT[s], r_KT[g]], writes=[r_bank])
                mask = maskF if (ti == 0 and j == 0) else maskA
                stt = ast[a]
                A("dve", lambda e: e.tensor_tensor(out=Sm[a][:, 0:512], in0=bE[:, :], in1=mask[:], op=ALU.add),
                  reads=[r_bE] + CONST, writes=[r_Sm[a]])
                A("dve", lambda e: e.tensor_tensor(out=Sm[a][:, 512:1024], in0=bO[:, :], in1=mask[:], op=ALU.add),
                  reads=[r_bO] + CONST, writes=[r_Sm[a]])
                A("dve", lambda e: e.tensor_reduce(out=stt[:, 0:4], in_=Sm[a][:, :].rearrange("p (e k) -> p e k", e=4),
                                                   axis=AX.X, op=ALU.max),
                  reads=[r_Sm[a]], writes=[r_ast[a]])
                A("dve", lambda e: e.tensor_scalar(out=stt[:, 4:8], in0=stt[:, 0:4], scalar1=-0.125, scalar2=None, op0=ALU.mult),
                  reads=[r_ast[a]], writes=[r_ast[a]])
                for ee in range(4):
                    A("act", lambda e, ee=ee: e.activation(out=Pm[a][:, ee * 256:(ee + 1) * 256], in_=Sm[a][:, ee * 256:(ee + 1) * 256],
                                                           func=AF.Exp, scale=0.125, bias=stt[:, 4 + ee:5 + ee],
                                                           accum_out=stt[:, 8 + ee:9 + ee]),
                      reads=[r_Sm[a], r_ast[a]], writes=[r_Pm[a], r_ast[a]])
                A("act", lambda e: e.activation(out=stt[:, 12:16], in_=stt[:, 4:8], func=AF.Exp),
                  reads=[r_ast[a]], writes=[r_ast[a]])
                A("dve", lambda e: e.tensor_tensor(out=stt[:, 12:16], in0=stt[:, 12:16], in1=esk[:, g * 4:g * 4 + 4], op=ALU.mult),
                  reads=[r_ast[a]] + CONST, writes=[r_ast[a]])
                A("dve", lambda e: e.tensor_tensor(out=stt[:, 12:16], in0=stt[:, 12:16], in1=stt[:, 8:12], op=ALU.add),
                  reads=[r_ast[a]], writes=[r_ast[a]])
                A("dve", lambda e: e.reciprocal(out=stt[:, 16:20], in_=stt[:, 12:16]),
                  reads=[r_ast[a]], writes=[r_ast[a]])
                for ee in range(4):
                    A("pool", lambda e, ee=ee: e.tensor_scalar(out=Pm[a][:, ee * 256:(ee + 1) * 256], in0=Pm[a][:, ee * 256:(ee + 1) * 256],
                                                               scalar1=stt[:, 16 + ee:17 + ee], scalar2=1.0, op0=ALU.mult, op1=ALU.mult),
                      reads=[r_Pm[a], r_ast[a]], writes=[r_Pm[a]])

            def stage_C(j):
                a = stA[j]
                pt, r_pt = next_pt()
                for kb in range(2):
                    for ee in range(4):
                        A("pe", lambda e, kb=kb, ee=ee: e.transpose(
                            out=pt[:, kb * 512 + ee * 128: kb * 512 + (ee + 1) * 128],
                            in_=Pm[a][:, ee * 256 + kb * 128: ee * 256 + (kb + 1) * 128], identity=ident[:]),
                          reads=[r_Pm[a]] + CONST, writes=[r_pt])
                copy_op(evac_engine(), PTs[a][:], pt[:, :], [r_pt], [r_PTs[a]])

            def stage_E(j):
                a = stA[j]
                bV, r_bV = next_pb()
                for half in range(2):
                    for kb in range(2):
                        A("pe", lambda e, half=half, kb=kb: e.matmul(
                            bV[64 * half:64 * half + 64, 0:256], lhsT=Vt[:, j + kb, g * 64:(g + 1) * 64],
                            rhs=PTs[a][:, kb * 512 + half * 256: kb * 512 + half * 256 + 256],
                            start=(kb == 0), stop=(kb == 1)),
                          reads=[r_Vt, r_PTs[a]], writes=[r_bV])
                A("dve", lambda e: e.tensor_tensor(
                    out=GA[:, 2 * g:2 * g + 2, j * 128:(j + 1) * 128],
                    in0=bV[:, 0:256].rearrange("p (c t) -> p c t", c=2),
                    in1=AG[s][:, :, j * 128:(j + 1) * 128], op=ALU.mult),
                  reads=[r_bV, r_AG[s]], writes=[r_GA])

            for j in range(NB):
                stage_A(j)
                if j >= 1:
                    stage_C(j - 1)
                    stage_E(j - 1)
            stage_C(NB - 1)
            stage_E(NB - 1)

        def hgrn_prep(hd, warm):
            s = hd % 2
            T1, T2, T3, T4, T5 = HT[s]
            R1, R2, R3, R4, R5 = r_HT[s]
            wH, r_wH = get_w(G_H + hd)
            bf, r_bf = next_pb()
            proj_fm(bf, r_bf, wH, r_wH, 0, 512)
            A("act", lambda e: e.activation(out=T1[:], in_=bf[:, 0:NT], func=AF.Sigmoid), reads=[r_bf], writes=[R1])
            A("dve", lambda e: e.tensor_scalar(out=T1[:], in0=T1[:], scalar1=oml[:, hd:hd + 1], scalar2=lb[:, hd:hd + 1],
                                               op0=ALU.mult, op1=ALU.add), reads=[R1] + CONST, writes=[R1])
            A("act", lambda e: e.activation(out=T2[:], in_=T1[:], func=AF.Ln), reads=[R1], writes=[R2])
            A("dve", lambda e: e.tensor_tensor_scan(out=T3[:], data0=rmask[:], data1=T2[:], initial=0.0,
                                                    op0=ALU.mult, op1=ALU.add), reads=[R2] + CONST, writes=[R3])
            A("pool", lambda e: e.tensor_scalar(out=T1[:], in0=T1[:], scalar1=-1.0, scalar2=1.0, op0=ALU.mult, op1=ALU.add),
              reads=[R1], writes=[R1])
            A("act", lambda e: e.activation(out=T2[:], in_=T3[:], func=AF.Exp), reads=[R3, R2], writes=[R2])
            A("act", lambda e: e.activation(out=T4[:], in_=T3[:], func=AF.Exp, scale=-1.0), reads=[R3], writes=[R4])
            A("dve", lambda e: e.tensor_copy(out=EB[hd][:], in_=T2[:, :].rearrange("p (c t) -> p c t", t=64)[:, :, 63]),
              reads=[R2], writes=[r_hd[hd]])
            A("dve", lambda e: e.tensor_tensor(out=KH[s][:], in0=T1[:], in1=T4[:], op=ALU.mult),
              reads=[R1, R4], writes=[r_KH[s]])
            bv, r_bv = next_pb()
            for blk in range(NB):
                for kc in range(KC):
                    A("pe", lambda e, kc=kc, blk=blk: e.matmul(
                        bv[:, blk * 128:(blk + 1) * 128], lhsT=hT[:, kc, blk * 128:(blk + 1) * 128],
                        rhs=wH[:, kc * 512 + 256: kc * 512 + 384], start=(kc == 0), stop=(kc == KC - 1)),
                      reads=[r_wH, r_hT], writes=[r_bv])
            A("act", lambda e: e.activation(out=VH[hd][:, :, :], in_=bv[:, 0:NB * 128].rearrange("p (b c) -> p b c", b=NB),
                                            func=AF.Copy), reads=[r_bv], writes=[r_hd[hd]])
            if not warm:
                bq, r_bq = next_pb()
                proj_fm(bq, r_bq, wH, r_wH, 128, 512)
                A("act", lambda e: e.activation(out=T5[:], in_=bq[:, 0:NT], func=AF.Silu), reads=[r_bq], writes=[R5])
                A("dve", lambda e: e.scalar_tensor_tensor(out=QH[hd][:], in0=T5[:], scalar=float(128 ** -0.5), in1=T2[:],
                                                          op0=ALU.mult, op1=ALU.mult), reads=[R5, R2], writes=[r_hd[hd]])
                bg, r_bg = next_pb()
                proj_fm(bg, r_bg, wH, r_wH, 384, 512)
                A("act", lambda e: e.activation(out=HGs[:, hd, :], in_=bg[:, 0:NT], func=AF.Silu),
                  reads=[r_bg], writes=[r_HGs[hd]])
            pt, r_pt = next_pt()
            for blk in range(NB):
                A("pe", lambda e, blk=blk: e.transpose(out=pt[:, blk * 128:(blk + 1) * 128],
                                                       in_=KH[s][:, blk * 128:(blk + 1) * 128], identity=ident[:]),
                  reads=[r_KH[s]] + CONST, writes=[r_pt])
            A("dve", lambda e: e.tensor_copy(out=KTok[hd][:, :, :], in_=pt[:, 0:NB * 128].rearrange("p (b c) -> p b c", b=NB)),
              reads=[r_pt], writes=[r_hd[hd]])
            if not warm:
                bA, r_bA = next_pb()
                for blk in range(NB):
                    A("pe", lambda e, blk=blk: e.matmul(bA[:, blk * 128:(blk + 1) * 128], lhsT=KH[s][:, blk * 128:(blk + 1) * 128],
                                                        rhs=QH[hd][:, blk * 128:(blk + 1) * 128], start=True, stop=True),
                      reads=[r_KH[s], r_hd[hd]], writes=[r_bA])
                A("dve", lambda e: e.tensor_tensor(
                    out=ATm[hd][:, :].rearrange("p (b t) -> p b t", b=NB),
                    in0=bA[:, 0:NT].rearrange("p (b t) -> p b t", b=NB),
                    in1=maskH[:, :].unsqueeze(1).to_broadcast([128, NB, 128]), op=ALU.mult),
                  reads=[r_bA] + CONST, writes=[r_hd[hd]])

        def hgrn_core(warm):
            for blk in range(NB):
                hgrn_core_blk(blk, warm)

        def hgrn_core_blk(blk, warm):
            banks = []
            if not warm:
                banks = [next_pb(), next_pb()]
                for hd in range(8):
                    b, r_b = banks[hd // 4]
                    q = hd % 4
                    A("pe", lambda e, hd=hd, b=b, q=q: e.matmul(
                        b[:, q * 128:(q + 1) * 128], lhsT=VH[hd][:, blk, :], rhs=ATm[hd][:, blk * 128:(blk + 1) * 128],
                        start=(q == 0), stop=False, skip_group_check=True),
                      reads=[r_hd[hd]], writes=[r_b])
            for half in range(2):
                hgrn_core_half(blk, half, warm, banks)
            if not warm:
                for bi in range(2):
                    hgrn_out(blk, bi, banks[bi], [banks[0][1], banks[1][1]])

        def hgrn_core_half(blk, half, warm, banks):
            c = blk * 2 + half
            lo = 64 * half
            if not warm:
                for hd in range(8):
                    b, r_b = banks[hd // 4]
                    q = hd % 4
                    A("pe", lambda e, hd=hd, b=b, q=q: e.matmul(
                        b[:, q * 128 + lo: q * 128 + lo + 64], lhsT=Sb[hd][half][:],
                        rhs=QH[hd][:, blk * 128 + lo: blk * 128 + lo + 64],
                        start=False, stop=(half == 1 and q == 3), skip_group_check=True),
                      reads=[r_hd[hd], r_Sb[hd][half]], writes=[r_b])
            sbanks = [next_pb(), next_pb()]
            for hd in range(8):
                bs, r_bs = sbanks[hd // 4]
                q = hd % 4
                A("pe", lambda e, hd=hd, bs=bs, q=q: e.matmul(
                    bs[:, q * 128:(q + 1) * 128], lhsT=KTok[hd][lo:lo + 64, blk, :], rhs=VH[hd][lo:lo + 64, blk, :],
                    start=True, stop=True), reads=[r_hd[hd]], writes=[r_bs])
            for hd in range(8):
                bs, r_bs = sbanks[hd // 4]
                q = hd % 4
                A("dve", lambda e, hd=hd, bs=bs, q=q: e.tensor_tensor(
                    out=Tst[hd][:], in0=bs[:, q * 128:(q + 1) * 128], in1=Sf[hd][:], op=ALU.add),
                  reads=[r_bs, r_Sf[hd]], writes=[r_Tst[hd]])
                A("dve", lambda e, hd=hd: e.tensor_scalar(out=Sf[hd][:], in0=Tst[hd][:], scalar1=EB[hd][:, c:c + 1],
                                                       scalar2=None, op0=ALU.mult),
                  reads=[r_Tst[hd], r_hd[hd]], writes=[r_Sf[hd]])
                A("act", lambda e, hd=hd: e.activation(out=Sb[hd][1 - half][:], in_=Tst[hd][:], func=AF.Identity,
                                                      scale=EB[hd][:, c:c + 1]),
                  reads=[r_Tst[hd], r_hd[hd]], writes=[r_Sb[hd][1 - half]])

        def hgrn_out(blk, bi, bank, live):
            b, r_b = bank
            a = (cnt.setdefault("ho", 0)) % 2
            cnt["ho"] += 1
            if os.environ.get("DBG_TAPS"):
                A("dve", lambda e: e.tensor_copy(out=ORAW[:], in_=b[:, :]), reads=[r_b], writes=[r_ORAW])
            A("act", lambda e: e.activation(out=SQ[a][:], in_=b[:, :], func=AF.Square),
              reads=[r_b], writes=[r_SQ[a]])
            bm, r_bm = next_pb(avoid=live)
            A("pe", lambda e: e.matmul(bm[:, :], lhsT=onesd[:], rhs=SQ[a][:], start=True, stop=True),
              reads=[r_SQ[a]] + CONST, writes=[r_bm])
            A("act", lambda e: e.activation(out=RR[a][:], in_=bm[:, :], func=AF.Ln, bias=EPS),
              reads=[r_bm], writes=[r_RR[a]])
            A("act", lambda e: e.activation(out=RR[a][:], in_=RR[a][:], func=AF.Exp, scale=-0.5),
              reads=[r_RR[a]], writes=[r_RR[a]])
            A("dve", lambda e: e.tensor_tensor(out=OT[a][:], in0=b[:, :], in1=RR[a][:], op=ALU.mult),
              reads=[r_b, r_RR[a]], writes=[r_OT[a]])
            for q in range(4):
                hd = bi * 4 + q
                A("dve", lambda e, hd=hd, q=q: e.scalar_tensor_tensor(
                    out=GH[:, hd, blk * 128:(blk + 1) * 128], in0=OT[a][:, q * 128:(q + 1) * 128],
                    scalar=hgain[:, hd:hd + 1], in1=HGs[:, hd, blk * 128:(blk + 1) * 128],
                    op0=ALU.mult, op1=ALU.mult),
                  reads=[r_OT[a], r_HGs[hd]] + CONST, writes=[r_GH])

        def merge_stage():
            for dc in range(16):
                merge_dc(dc)

        def merge_dc(dc):
            if True:
                a = dc % 2
                wM, r_wM = get_w(G_M + dc, 6144)
                bya, r_bya = next_pb()
                for kc in range(8):
                    A("pe", lambda e, kc=kc: e.matmul(bya[:, 0:NT], lhsT=wM[:, 4096 + kc * 128: 4096 + (kc + 1) * 128],
                                                      rhs=GA[:, kc, :], start=(kc == 0), stop=(kc == 7)),
                      reads=[r_wM, r_GA], writes=[r_bya])
                bma, r_bma = next_pb()
                for kc in range(KC):
                    A("pe", lambda e, kc=kc: e.matmul(bma[:, 0:NT], lhsT=wM[:, kc * 128:(kc + 1) * 128],
                                                      rhs=hT[:, kc, :], start=(kc == 0), stop=(kc == KC - 1)),
                      reads=[r_wM, r_hT], writes=[r_bma])
                byh, r_byh = next_pb()
                for kc in range(8):
                    A("pe", lambda e, kc=kc: e.matmul(byh[:, 0:NT], lhsT=wM[:, 5120 + kc * 128: 5120 + (kc + 1) * 128],
                                                      rhs=GH[:, kc, :], start=(kc == 0), stop=(kc == 7)),
                      reads=[r_wM, r_GH], writes=[r_byh])
                bmh, r_bmh = next_pb()
                for kc in range(KC):
                    A("pe", lambda e, kc=kc: e.matmul(bmh[:, 0:NT], lhsT=wM[:, 2048 + kc * 128: 2048 + (kc + 1) * 128],
                                                      rhs=hT[:, kc, :], start=(kc == 0), stop=(kc == KC - 1)),
                      reads=[r_wM, r_hT], writes=[r_bmh])
                m0, m1, m2, m3 = mt[a]
                q0, q1, q2, q3 = r_mt[a]
                A("act", lambda e: e.activation(out=m0[:], in_=bma[:, 0:NT], func=AF.Sigmoid), reads=[r_bma], writes=[q0])
                A("act", lambda e: e.activation(out=m1[:], in_=bmh[:, 0:NT], func=AF.Sigmoid), reads=[r_bmh], writes=[q1])
                A("dve", lambda e: e.tensor_tensor(out=m2[:], in0=bya[:, 0:NT], in1=m0[:], op=ALU.mult),
                  reads=[r_bya, q0], writes=[q2])
                A("dve", lambda e: e.tensor_tensor(out=m3[:], in0=byh[:, 0:NT], in1=m1[:], op=ALU.mult),
                  reads=[r_byh, q1], writes=[q3])
                A("pool", lambda e: e.tensor_tensor(out=MG[:, dc, :], in0=m2[:], in1=m3[:], op=ALU.add),
                  reads=[q2, q3], writes=[r_MG])

        out_events = []

        def final_load(ti):
            row0 = (ti + 1) * NT
            for blk in range(NB):
                r0 = row0 + blk * 128
                A("sp", lambda e, blk=blk, r0=r0: e.dma_start(out=fin[blk][:], in_=x_d[r0:r0 + 128, :]),
                  writes=[r_fin[blk]], chan="fin%d" % blk)

        def final_stage(ti):
            for cg in range(4):
                final_cg(cg)

        def final_cg(cg):
            if True:
                wO, r_wO = get_w(G_O + cg)
                for blk in range(NB):
                    bank, r_bank = next_pb()
                    for kc in range(KC):
                        A("pe", lambda e, kc=kc, blk=blk, bank=bank: e.matmul(
                            bank[:, :], lhsT=MG[:, kc, blk * 128:(blk + 1) * 128], rhs=wO[:, kc * 512:(kc + 1) * 512],
                            start=(kc == 0), stop=(kc == KC - 1)), reads=[r_wO, r_MG], writes=[r_bank])
                    A("dve", lambda e, blk=blk, bank=bank, cg=cg: e.tensor_tensor(
                        out=fin[blk][:, cg * 512:(cg + 1) * 512], in0=bank[:, :], in1=fin[blk][:, cg * 512:(cg + 1) * 512],
                        op=ALU.add), reads=[r_bank, r_fin[blk]], writes=[r_fin[blk]])
                    A("act", lambda e, blk=blk, cg=cg: e.activation(out=junk[:], in_=fin[blk][:, cg * 512:(cg + 1) * 512],
                                                                    func=AF.Square, accum_out=fst[blk][:, 4 + cg:5 + cg]),
                      reads=[r_fin[blk]], writes=[r_junk, r_fst[blk]])

        def final_norm(ti):
            for blk in range(NB):
                f = fst[blk]
                A("dve", lambda e, f=f: e.tensor_reduce(out=f[:, 0:1], in_=f[:, 4:8], axis=AX.X, op=ALU.add),
                  reads=[r_fst[blk]], writes=[r_fst[blk]])
                A("act", lambda e, f=f: e.activation(out=f[:, 1:2], in_=f[:, 0:1], func=AF.Ln, scale=1.0 / D, bias=EPS),
                  reads=[r_fst[blk]], writes=[r_fst[blk]])
                A("act", lambda e, f=f: e.activation(out=f[:, 2:3], in_=f[:, 1:2], func=AF.Exp, scale=-0.5),
                  reads=[r_fst[blk]], writes=[r_fst[blk]])
                A("dve", lambda e, blk=blk, f=f: e.scalar_tensor_tensor(out=fin[blk][:], in0=fin[blk][:], scalar=f[:, 2:3],
                                                                      in1=fgain[:], op0=ALU.mult, op1=ALU.mult),
                  reads=[r_fin[blk], r_fst[blk]] + CONST, writes=[r_fin[blk]])
                r0 = ti * NT + blk * 128
                A("pool", lambda e, blk=blk, r0=r0: e.dma_start(out=out_d[r0:r0 + 128, :], in_=fin[blk][:]),
                  reads=[r_fin[blk]], writes=[r_fin[blk]], chan="fin%d" % blk)


        STG = int(os.environ.get("DBG_STAGE", "99"))
        if STG >= 1:
            prologue_p1(-1)
        if STG >= 2:
            prologue_p2()
        for ti in range(-1, NTILES):
            if STG < 3:
                break
            warm = ti < 0
            attn_kv(ti)
            if STG < 4:
                break
            if not warm:
                attn_proj(0)
                for g in range(4):
                    if g + 1 < 4:
                        attn_proj(g + 1)
                    if STG >= 5:
                        attn_core(g, ti)
            attn_halo()
            if STG < 6:
                continue
            for hd in range(8):
                hgrn_prep(hd, warm)
            if STG < 7:
                continue
            hgrn_core(warm)
            if STG < 8:
                continue
            if not warm:
                final_load(ti)
                merge_stage()
            if ti + 1 < NTILES:
                prologue_p1(ti + 1)
            if STG < 9:
                continue
            if not warm:
                final_stage(ti)
            if ti + 1 < NTILES:
                prologue_p2()
            if not warm:
                final_norm(ti)
        taps = []
        if os.environ.get("DBG_TAPS"):
            def tap(name, t, shape, dt, rl):
                d = nc.dram_tensor("tap_" + name, shape, dt, kind="ExternalOutput").ap()
                rr = Res("tap_" + name)
                A("sp", lambda e: e.dma_start(out=d, in_=t[:]), reads=rl, writes=[rr], chan="tap_" + name)
                taps.append(rr)
            tap("hT", hT, [128, KC, NT], BF16, [r_hT])
            for g in range(4):
                tap("KT%d" % g, KT[g], [128, 128 + NT], BF16, [r_KT[g]])
            tap("Vt", Vt, [128, NB + 1, 256], BF16, [r_Vt])
            tap("QT0", QT[0], [128, 2, NT], BF16, [r_QT[0]])
            tap("QT1", QT[1], [128, 2, NT], BF16, [r_QT[1]])
            tap("AG1", AG[1], [128, 2, NT], BF16, [r_AG[1]])
            tap("GA", GA, [128, 8, NT], BF16, [r_GA])
            tap("GH", GH, [128, 8, NT], BF16, [r_GH])
            tap("HGs", HGs, [128, 8, NT], BF16, r_HGs)
            tap("MG", MG, [128, KC, NT], BF16, [r_MG])
            tap("cosT", cosT, [128, NT], F32, [r_rope])
            tap("sinT", sinT, [128, NT], F32, [r_rope])
            tap("Sf0", Sf[0], [128, 128], F32, [r_Sf[0]])
            tap("ORAW", ORAW, [128, 512], F32, [r_ORAW])
            tap("OT", OT[0], [128, 512], F32, [r_OT[0]])
            tap("RR", RR[0], [128, 512], F32, [r_RR[0]])
            tap("Sb0", Sb[0][0], [128, 128], BF16, [r_Sb[0][0]])
            tap("QH0", QH[0], [128, NT], BF16, [r_hd[0]])
            tap("VH0", VH[0], [128, NB, 128], BF16, [r_hd[0]])
            tap("KTok0", KTok[0], [128, NB, 128], BF16, [r_hd[0]])
            tap("ATm0", ATm[0], [128, NT], BF16, [r_hd[0]])
            tap("EB0", EB[0], [128, NCH], F32, [r_hd[0]])
            A("sp", lambda e: e.nop(), reads=taps)
        A("pool", lambda e: e.nop(), reads=r_fin)

        S.prepare()
        with nc.Block() as block:
            @block.tensor
            def _(e):
                S.emit_one("pe", e)

            @block.scalar
            def _(e):
                S.emit_one("act", e)

            @block.vector
            def _(e):
                S.emit_one("dve", e)

            @block.gpsimd
            def _(e):
                S.emit_one("pool", e)

            @block.sync
            def _(e):
                S.emit_one("sp", e)
    return nc


def _consts():
    bf = ml_dtypes.bfloat16
    ident = np.eye(128, dtype=np.float32).astype(bf)
    onesd = np.full((128, 128), 1.0 / 128.0, dtype=np.float32).astype(bf)
    prot = np.zeros((128, 128), dtype=np.float32)
    for m in range(128):
        p = m + 32 if (m % 64) < 32 else m - 32
        prot[p, m] = 1.0
    prot = prot.astype(bf)
    q = np.arange(128)[:, None]
    k = np.arange(256)[None, :]
    rel = (q + 128) - k
    band = (rel >= 0) & (rel < 128)
    m1 = np.where(band, 0.0, -30000.0).astype(np.float32)
    maskA = np.concatenate([m1, m1], axis=1)
    mf = m1.copy()
    mf[:, :128] = -30000.0
    maskF0 = np.concatenate([mf, mf], axis=1)
    s = np.arange(128)[:, None]
    t = np.arange(128)[None, :]
    maskH = (((s // 64) == (t // 64)) & (s <= t)).astype(np.float32)
    rmask = np.ones((128, NT), dtype=np.float32)
    rmask[:, ::64] = 0.0
    half = 32
    inv_freq = (10000.0 ** (-np.arange(half, dtype=np.float32) / half)).astype(np.float32)
    p = np.arange(128)
    sgn = np.where((p % 64) < 32, -1.0, 1.0)
    invf = (sgn * inv_freq[p % 32].astype(np.float64) / (2.0 * np.pi)).astype(np.float32)[:, None]
    return dict(ident=ident, onesd=onesd, prot=prot, maskA=maskA, maskF0=maskF0, maskH=maskH, rmask=rmask, invf=invf)


_PROGRAM = None


def make_in_maps(inp, ncore=NCORE, seg=SEG):
    x = np.asarray(inp["x"], dtype=np.float32)
    positions = np.asarray(inp["positions"], dtype=np.int32)
    c = _consts()
    w_in0 = np.ascontiguousarray(np.asarray(inp["w_in"], dtype=np.float32)[0])
    w_ao0 = np.ascontiguousarray(np.asarray(inp["w_attn_out"], dtype=np.float32)[0])
    w_ho0 = np.ascontiguousarray(np.asarray(inp["w_hgrn_out"], dtype=np.float32)[0])
    w_o0 = np.ascontiguousarray(np.asarray(inp["w_o"], dtype=np.float32)[0])
    gin = np.ascontiguousarray(np.asarray(inp["norm_gain"], dtype=np.float32)[0].reshape(KC, 128).T)
    fgain = np.asarray(inp["final_norm_gain"], dtype=np.float32).reshape(1, D)
    sinks = np.asarray(inp["attn_sinks"], dtype=np.float32)[0]
    sink_perm = np.array([sinks[4 * g + EPERM[e]] for g in range(4) for e in range(4)], dtype=np.float32).reshape(1, 16)
    lbr = np.asarray(inp["hgrn_lower_bounds"], dtype=np.float32)
    lbraw = np.ascontiguousarray(lbr.reshape(2, 8, 128).transpose(2, 0, 1))
    hgain = np.ascontiguousarray(np.asarray(inp["hgrn_norm_gain"], dtype=np.float32)[0].T)
    nseg = x.shape[1] // seg
    in_maps = []
    for core in range(ncore):
        b, s = core // nseg, core % nseg
        t0 = s * seg
        xe = np.zeros((WARM + seg, D), dtype=np.float32)
        pe = np.zeros((1, WARM + seg), dtype=np.int32)
        xe[WARM:] = x[b, t0:t0 + seg]
        pe[0, WARM:] = positions[b, t0:t0 + seg]
        if s > 0:
            xe[:WARM] = x[b, t0 - WARM:t0]
            pe[0, :WARM] = positions[b, t0 - WARM:t0]
        in_maps.append({
            "x": xe, "pos": pe, "w_in": w_in0, "w_ao": w_ao0, "w_ho": w_ho0, "w_o": w_o0,
            "ident": c["ident"], "onesd": c["onesd"], "prot": c["prot"], "maskA": c["maskA"],
            "maskF": c["maskF0"] if s == 0 else c["maskA"], "maskH": c["maskH"], "rmask": c["rmask"],
            "invf": c["invf"], "gin": gin, "fgain": fgain, "sink": sink_perm, "lbraw": lbraw, "hgain": hgain,
        })
    return in_maps


def kernel(x, positions, norm_gain, w_in, attn_sinks, hgrn_lower_bounds, hgrn_norm_gain,
           w_attn_out, w_hgrn_out, w_o, final_norm_gain):
    global _PROGRAM
    in_maps = make_in_maps(dict(x=x, positions=positions, norm_gain=norm_gain, w_in=w_in, attn_sinks=attn_sinks,
                                hgrn_lower_bounds=hgrn_lower_bounds, hgrn_norm_gain=hgrn_norm_gain,
                                w_attn_out=w_attn_out, w_hgrn_out=w_hgrn_out, w_o=w_o,
                                final_norm_gain=final_norm_gain))
    if _PROGRAM is None:
        _PROGRAM = build_program()
    res = run_bass_kernel_spmd(_PROGRAM, in_maps, core_ids=list(range(NCORE)))
    out = np.empty((2, SEQ, D), dtype=np.float32)
    for core in range(NCORE):
        b, s = core // 4, core % 4
        out[b, s * SEG:(s + 1) * SEG] = np.asarray(res.results[core]["out"], dtype=np.float32)
    return out
```

```python
import os
import numpy as np
import ml_dtypes
from contextlib import ExitStack
import concourse.bass as bass
import concourse.mybir as mybir
from concourse.bass_utils import run_bass_kernel_spmd

F32 = mybir.dt.float32
BF16 = mybir.dt.bfloat16
I32 = mybir.dt.int32
AF = mybir.ActivationFunctionType
ALU = mybir.AluOpType
AX = mybir.AxisListType

SAME_ENG_SYNC = True

D = 2048
SEQ = 16384
NCORE = 8
SEG = 4096
NT = 256
NB = NT // 128
NCH = NT // 64
NTILES = SEG // NT
WARM = NT
KC = 16
EPS = 1e-6
TWO_PI = 6.2831845

OFF_AQ, OFF_AK, OFF_AV, OFF_AG = 0, 1024, 1280, 1536
OFF_HQ, OFF_HF, OFF_HI, OFF_HG = 2560, 3584, 4608, 5632
OFF_MA, OFF_MH = 6656, 8704
IN_W = 10752

G_K, G_V = 0, 1
G_QA = 2
G_H = 6
G_M = 14
G_O = 30
NGRP = 34
SLOT = 8192
NSLOT = 3
EPERM = (0, 2, 1, 3)


class Res:
    __slots__ = ("name", "writer", "readers", "const", "excl")

    def __init__(self, name, const=False, excl=False):
        self.name = name
        self.writer = None
        self.readers = []
        self.const = const
        self.excl = excl


class Op:
    __slots__ = ("eng", "fn", "deps", "marked", "event", "chan")

    def __init__(self, eng, fn, chan=None):
        self.eng = eng
        self.fn = fn
        self.deps = []
        self.marked = False
        self.event = None
        self.chan = chan


class Sched:
    ENG = ("pe", "act", "dve", "pool", "sp")

    def __init__(self, nc, stack):
        self.nc = nc
        self.stack = stack
        self.ops = {e: [] for e in self.ENG}
        self.sems = {}
        self.chan_cnt = {}
        self.n_wait = 0

    def sem(self, key):
        if key not in self.sems:
            name = "s_" + "_".join(str(k) for k in key)
            self.sems[key] = self.stack.enter_context(self.nc.semaphore(name))
        return self.sems[key]

    def add(self, eng, fn, reads=(), writes=(), chan=None):
        op = Op(eng, fn, chan)
        if any(r.excl for r in reads):
            writes = list(writes) + [r for r in reads if r.excl]
            reads = [r for r in reads if not r.excl]
        deps = {}
        for r in reads:
            if r.writer is not None:
                deps[id(r.writer)] = r.writer
        for w in writes:
            if w.writer is not None:
                deps[id(w.writer)] = w.writer
            for rd in w.readers:
                deps[id(rd)] = rd
        for r in reads:
            if not r.const:
                r.readers.append(op)
        for w in writes:
            w.writer = op
            w.readers = []
        dl = []
        for d in deps.values():
            if d is op:
                continue
            if d.eng == eng and d.chan is None and chan is None:
                if eng == "pe" or not SAME_ENG_SYNC:
                    continue
            dl.append(d)
        op.deps = dl
        if chan is not None:
            n = self.chan_cnt.get(chan, 0) + 16
            self.chan_cnt[chan] = n
            op.event = (("D", chan), n)
            op.marked = True
        self.ops[eng].append(op)
        return op

    def prepare(self):
        for e in self.ENG:
            for op in self.ops[e]:
                for d in op.deps:
                    d.marked = True
        for e in self.ENG:
            c = 0
            for op in self.ops[e]:
                if op.chan is None and op.marked:
                    c += 1
                    op.event = (("E", e), c)
        for e in self.ENG:
            for op in self.ops[e]:
                if op.event is not None:
                    self.sem(op.event[0])

    def emit_one(self, e, eng):
        seen = {}
        for op in self.ops[e]:
            waits = {}
            for d in op.deps:
                k, v = d.event
                if waits.get(k, 0) < v:
                    waits[k] = v
            for k, v in waits.items():
                if seen.get(k, 0) >= v:
                    continue
                seen[k] = v
                eng.wait_ge(self.sem(k), v)
                self.n_wait += 1
            inst = op.fn(eng)
            if op.chan is not None:
                inst.then_inc(self.sem(op.event[0]), 16)
            elif op.marked:
                inst.then_inc(self.sem(op.event[0]), 1)


def build_program(NTILES=NTILES):
    SEG = NTILES * NT
    nc = bass.Bass("TRN2", target_bir_lowering=False)

    def din(name, shape, dt):
        return nc.dram_tensor(name, shape, dt, kind="ExternalInput").ap()

    x_d = din("x", [WARM + SEG, D], F32)
    pos_d = din("pos", [1, WARM + SEG], I32)
    w_in_d = din("w_in", [D, IN_W], F32)
    w_ao_d = din("w_ao", [1024, D], F32)
    w_ho_d = din("w_ho", [1024, D], F32)
    w_o_d = din("w_o", [D, D], F32)
    ident_d = din("ident", [128, 128], BF16)
    onesd_d = din("onesd", [128, 128], BF16)
    prot_d = din("prot", [128, 128], BF16)
    maskA_d = din("maskA", [128, 512], F32)
    maskF_d = din("maskF", [128, 512], F32)
    maskH_d = din("maskH", [128, 128], F32)
    rmask_d = din("rmask", [128, NT], F32)
    invf_d = din("invf", [128, 1], F32)
    gin_d = din("gin", [128, KC], F32)
    fgain_d = din("fgain", [1, D], F32)
    sink_d = din("sink", [1, 16], F32)
    lbraw_d = din("lbraw", [128, 2, 8], F32)
    hgain_d = din("hgain", [128, 8], F32)
    out_d = nc.dram_tensor("out", [SEG, D], F32, kind="ExternalOutput").ap()
    wsc_d = nc.dram_tensor("wsc", [NGRP, 128, SLOT], BF16, kind="Internal").ap()

    with ExitStack() as st:
        S = Sched(nc, st)
        A = S.add

        def sb(name, shape, dt):
            return st.enter_context(nc.sbuf_tensor("sb_" + name, shape, dt))

        def ps(name, shape, dt):
            return st.enter_context(nc.psum_tensor("ps_" + name, shape, dt))

        ident = sb("ident", [128, 128], BF16)
        onesd = sb("onesd", [128, 128], BF16)
        prot = sb("prot", [128, 128], BF16)
        maskA = sb("maskA", [128, 512], F32)
        maskF = sb("maskF", [128, 512], F32)
        maskH = sb("maskH", [128, 128], F32)
        rmask = sb("rmask", [128, NT], F32)
        invf = sb("invf", [128, 1], F32)
        gin = sb("gin", [128, KC], F32)
        fgain = sb("fgain", [128, D], F32)
        sinkb = sb("sinkb", [128, 16], F32)
        esk = sb("esk", [128, 16], F32)
        lbraw = sb("lbraw", [128, 2, 8], F32)
        lbd = sb("lbd", [128, 8], F32)
        lb = sb("lb", [128, 8], F32)
        oml = sb("oml", [128, 8], F32)
        hgain = sb("hgain", [128, 8], F32)
        r_const = Res("const", const=True)
        cdma = [(ident, ident_d), (onesd, onesd_d), (prot, prot_d), (maskA, maskA_d), (maskF, maskF_d),
                (maskH, maskH_d), (rmask, rmask_d), (invf, invf_d), (gin, gin_d), (lbraw, lbraw_d),
                (hgain, hgain_d)]
        for t_, d_ in cdma:
            A("sp", lambda e, t_=t_, d_=d_: e.dma_start(out=t_[:], in_=d_), writes=[r_const], chan="const")
        A("sp", lambda e: e.dma_start(out=fgain[:], in_=fgain_d[0:1, :].to_broadcast([128, D])),
          writes=[r_const], chan="const")
        A("sp", lambda e: e.dma_start(out=sinkb[:], in_=sink_d[0:1, :].to_broadcast([128, 16])),
          writes=[r_const], chan="const")
        r_c2 = Res("const2")
        A("dve", lambda e: e.tensor_tensor(out=lbd[:], in0=lbraw[:, 0, :], in1=lbraw[:, 1, :], op=ALU.subtract),
          reads=[r_const], writes=[r_c2])
        A("act", lambda e: e.activation(out=lb[:], in_=lbd[:], func=AF.Sigmoid), reads=[r_c2], writes=[r_c2])
        A("act", lambda e: e.activation(out=oml[:], in_=lbd[:], func=AF.Sigmoid, scale=-1.0), reads=[r_c2], writes=[r_c2])
        A("act", lambda e: e.activation(out=esk[:], in_=sinkb[:], func=AF.Exp), reads=[r_const], writes=[r_c2])
        r_c2.const = True
        CONST = [r_const, r_c2]

        PB = [ps("pb%d" % i, [128, 512], F32) for i in range(6)]
        r_PB = [Res("pb%d" % i, excl=True) for i in range(6)]
        PT = [ps("pt%d" % i, [128, 1024], BF16) for i in range(2)]
        r_PT = [Res("pt%d" % i, excl=True) for i in range(2)]
        cnt = {"pb": 0, "pt": 0, "ws": 0, "ev": 0}

        def next_pb(avoid=()):
            while True:
                i = cnt["pb"] % 6
                cnt["pb"] += 1
                if not any(r_PB[i] is r for r in avoid):
                    return PB[i], r_PB[i]

        def next_pt():
            i = cnt["pt"] % 2
            cnt["pt"] += 1
            return PT[i], r_PT[i]

        r_wsc = [Res("wsc%d" % g) for g in range(NGRP)]

        def cast(g, dst_lo, ncols, src, col0, nk):
            dst = wsc_d[g][:, dst_lo:dst_lo + nk * ncols].rearrange("p (k c) -> p k c", k=nk)
            s_ = src[:, col0:col0 + ncols].rearrange("(k p) c -> p k c", p=128)
            A("pool", lambda e: e.dma_start(out=dst, in_=s_), writes=[r_wsc[g]], chan="wsc%d" % g)

        def cast_sub(g, width, sub_lo, ncols, src, col0, nk):
            dst = wsc_d[g][:, 0:nk * width].rearrange("p (k c) -> p k c", k=nk)[:, :, sub_lo:sub_lo + ncols]
            s_ = src[:, col0:col0 + ncols].rearrange("(k p) c -> p k c", p=128)
            A("pool", lambda e: e.dma_start(out=dst, in_=s_), writes=[r_wsc[g]], chan="wsc%d" % g)

        def cast_group(g):
            if g == G_K:
                for kv in range(4):
                    for dup in range(2):
                        cast_sub(G_K, 512, kv * 128 + dup * 64, 64, w_in_d, OFF_AK + kv * 64, KC)
            elif g == G_V:
                cast_sub(G_V, 256, 0, 256, w_in_d, OFF_AV, KC)
            elif g < G_H:
                kv = g - G_QA
                cast_sub(g, 512, 0, 256, w_in_d, OFF_AQ + kv * 256, KC)
                cast_sub(g, 512, 256, 256, w_in_d, OFF_AG + kv * 256, KC)
            elif g < G_M:
                hd = g - G_H
                for q, off in enumerate((OFF_HF, OFF_HQ, OFF_HI, OFF_HG)):
                    cast_sub(g, 512, q * 128, 128, w_in_d, off + hd * 128, KC)
            elif g < G_O:
                dc = g - G_M
                cast(g, 0, 128, w_in_d, OFF_MA + dc * 128, KC)
                cast(g, 2048, 128, w_in_d, OFF_MH + dc * 128, KC)
                cast(g, 4096, 128, w_ao_d, dc * 128, 8)
                cast(g, 5120, 128, w_ho_d, dc * 128, 8)
            else:
                cg = g - G_O
                cast_sub(g, 512, 0, 512, w_o_d, cg * 512, KC)

        cast_order = [G_K, G_V] + [G_H + h for h in range(8)] + [G_QA + g for g in range(4)] \
            + [G_M + d for d in range(16)] + [G_O + c for c in range(4)]
        cast_done = set()
        CAST_AHEAD = 4

        def ensure_cast(g):
            pos = cast_order.index(g)
            for gg in cast_order[:pos + 1 + CAST_AHEAD]:
                if gg not in cast_done:
                    cast_done.add(gg)
                    cast_group(gg)

        WS = [sb("ws%d" % i, [128, SLOT], BF16) for i in range(NSLOT)]
        r_WS = [Res("ws%d" % i) for i in range(NSLOT)]

        def get_w(g, nelem=SLOT):
            ensure_cast(g)
            i = cnt["ws"] % NSLOT
            cnt["ws"] += 1
            A("sp", lambda e: e.dma_start(out=WS[i][:, 0:nelem], in_=wsc_d[g][:, 0:nelem]),
              reads=[r_wsc[g]], writes=[r_WS[i]], chan="ws%d" % i)
            return WS[i], r_WS[i]

        xs = sb("xs", [128, D], F32); r_xs = Res("xs")
        hn = [sb("hn%d" % i, [128, D], BF16) for i in range(NB)]
        r_hn = [Res("hn%d" % i) for i in range(NB)]
        st_small = [sb("stt%d" % i, [128, 4], F32) for i in range(NB)]
        r_sts = [Res("stt%d" % i) for i in range(NB)]
        hT = sb("hT", [128, KC, NT], BF16); r_hT = Res("hT")
        posi = sb("posi", [128, NT], I32); r_posi = Res("posi")
        rp = [sb("rp%d" % i, [128, NT], F32) for i in range(4)]
        r_rp = [Res("rp%d" % i) for i in range(4)]
        cosT = sb("cosT", [128, NT], F32); sinT = sb("sinT", [128, NT], F32)
        r_rope = Res("rope")
        KT = [sb("KT%d" % g, [128, 128 + NT], BF16) for g in range(4)]
        r_KT = [Res("KT%d" % g) for g in range(4)]
        Vt = sb("Vt", [128, NB + 1, 256], BF16); r_Vt = Res("Vt")
        QT = [sb("QT%d" % i, [128, 2, NT], BF16) for i in range(2)]
        r_QT = [Res("QT%d" % i) for i in range(2)]
        AG = [sb("AG%d" % i, [128, 2, NT], BF16) for i in range(2)]
        r_AG = [Res("AG%d" % i) for i in range(2)]
        GA = sb("GA", [128, 8, NT], BF16); r_GA = Res("GA")
        GH = sb("GH", [128, 8, NT], BF16); r_GH = Res("GH")
        HGs = sb("HGs", [128, 8, NT], BF16); r_HGs = [Res("HGs%d" % h) for h in range(8)]
        MG = sb("MG", [128, KC, NT], BF16); r_MG = Res("MG")
        qraw = [sb("qraw%d" % i, [128, NT], BF16) for i in range(2)]
        r_qraw = [Res("qraw%d" % i) for i in range(2)]
        rt1 = [sb("rt1_%d" % i, [128, NT], F32) for i in range(2)]
        rt2 = [sb("rt2_%d" % i, [128, NT], F32) for i in range(2)]
        r_rt = [Res("rt%d" % i) for i in range(2)]
        Sm = [sb("Sm0", [128, 1024], F32)] * 2
        r_Sm = [Res("Sm0")] * 2
        Pm = [sb("Pm%d" % i, [128, 1024], BF16) for i in range(2)]
        r_Pm = [Res("Pm%d" % i) for i in range(2)]
        PTs = [sb("PTs%d" % i, [128, 1024], BF16) for i in range(2)]
        r_PTs = [Res("PTs%d" % i) for i in range(2)]
        ast = [sb("ast%d" % i, [128, 24], F32) for i in range(2)]
        r_ast = [Res("ast%d" % i) for i in range(2)]
        Sf = [sb("Sf%d" % h, [128, 128], F32) for h in range(8)]
        r_Sf = [Res("Sf%d" % h) for h in range(8)]
        Sb = [[sb("Sb%d_%d" % (h, p), [128, 128], BF16) for p in range(2)] for h in range(8)]
        r_Sb = [[Res("Sb%d_%d" % (h, p)) for p in range(2)] for h in range(8)]
        QH = [sb("QH%d" % h, [128, NT], BF16) for h in range(8)]
        VH = [sb("VH%d" % h, [128, NB, 128], BF16) for h in range(8)]
        KTok = [sb("KTok%d" % h, [128, NB, 128], BF16) for h in range(8)]
        ATm = [sb("ATm%d" % h, [128, NT], BF16) for h in range(8)]
        EB = [sb("EB%d" % h, [128, NCH], F32) for h in range(8)]
        r_hd = [Res("hd%d" % h) for h in range(8)]
        HT = [[sb("HT0_%d" % k, [128, NT], F32) for k in range(5)]] * 2
        r_HT = [[Res("HT0_%d" % k) for k in range(5)]] * 2
        KH = [sb("KH%d" % i, [128, NT], BF16) for i in range(2)]
        r_KH = [Res("KH%d" % i) for i in range(2)]
        Tst = [sb("Tst%d" % h, [128, 128], F32) for h in range(4)] * 2
        r_Tst = [Res("Tst%d" % h) for h in range(4)] * 2
        SQ = [sb("SQ0", [128, 512], BF16)] * 2
        r_SQ = [Res("SQ0")] * 2
        RR = [sb("RR0", [128, 512], F32)] * 2
        r_RR = [Res("RR0")] * 2
        OT = [sb("OT%d" % i, [128, 512], F32) for i in range(2)]
        r_OT = [Res("OT%d" % i) for i in range(2)]
        if os.environ.get("DBG_TAPS"):
            ORAW = sb("ORAW", [128, 512], F32); r_ORAW = Res("ORAW")
        mt = [[sb("mt0_%d" % k, [128, NT], F32) for k in range(4)]] * 2
        r_mt = [[Res("mt0_%d" % k) for k in range(4)]] * 2
        fin = [sb("fin%d" % i, [128, D], F32) for i in range(NB)]
        r_fin = [Res("fin%d" % i) for i in range(NB)]
        junk = sb("junk", [128, 512], BF16); r_junk = Res("junk")
        fst = [sb("fst%d" % i, [128, 8], F32) for i in range(NB)]
        r_fst = [Res("fst%d" % i) for i in range(NB)]

        for h in range(8):
            A("pool", lambda e, h=h: e.memset(Sf[h][:], 0.0), writes=[r_Sf[h]])
            A("pool", lambda e, h=h: e.memset(Sb[h][0][:], 0.0), writes=[r_Sb[h][0]])
        for g in range(4):
            A("pool", lambda e, g=g: e.memset(KT[g][:], 0.0), writes=[r_KT[g]])
        A("pool", lambda e: e.memset(Vt[:], 0.0), writes=[r_Vt])

        def evac_engine():
            cnt["ev"] += 1
            return "act" if cnt["ev"] % 2 == 0 else "dve"

        def copy_op(eng, out, in_, reads, writes):
            if eng == "act":
                A("act", lambda e: e.activation(out=out, in_=in_, func=AF.Copy), reads=reads, writes=writes)
            else:
                A(eng, lambda e: e.tensor_copy(out=out, in_=in_), reads=reads, writes=writes)

        def prologue_p1(ti):
            row0 = (ti + 1) * NT
            for blk in range(NB):
                r0 = row0 + blk * 128
                A("act", lambda e, r0=r0: e.dma_start(out=xs[:], in_=x_d[r0:r0 + 128, :]), writes=[r_xs], chan="xs")
                stt = st_small[blk]
                A("act", lambda e, blk=blk, stt=stt: e.activation(out=hn[blk][:], in_=xs[:], func=AF.Square,
                                                                accum_out=stt[:, 0:1]),
                  reads=[r_xs], writes=[r_hn[blk], r_sts[blk]])
                A("act", lambda e, stt=stt: e.activation(out=stt[:, 1:2], in_=stt[:, 0:1], func=AF.Ln,
                                                         scale=1.0 / D, bias=EPS),
                  reads=[r_sts[blk]], writes=[r_sts[blk]])
                A("act", lambda e, stt=stt: e.activation(out=stt[:, 2:3], in_=stt[:, 1:2], func=AF.Exp, scale=-0.5),
                  reads=[r_sts[blk]], writes=[r_sts[blk]])
                A("dve", lambda e, blk=blk, stt=stt: e.tensor_scalar(out=hn[blk][:], in0=xs[:], scalar1=stt[:, 2:3],
                                                                  scalar2=None, op0=ALU.mult),
                  reads=[r_xs, r_sts[blk]], writes=[r_hn[blk]])
            A("act", lambda e: e.dma_start(out=posi[:], in_=pos_d[0:1, row0:row0 + NT].to_broadcast([128, NT])),
              writes=[r_posi], chan="posi")
            v, kf, tt, uc = rp[0], rp[1], rp[2], rp[3]
            RP = [r_posi] + r_rp
            P = "dve"
            A(P, lambda e: e.tensor_copy(out=v[:], in_=posi[:]), reads=RP + [r_rope], writes=RP)
            A(P, lambda e: e.tensor_scalar(out=v[:], in0=v[:], scalar1=invf[:, 0:1], scalar2=1.0, op0=ALU.mult, op1=ALU.mult),
              reads=RP + CONST, writes=RP)
            A(P, lambda e: e.tensor_copy(out=posi[:], in_=v[:]), reads=RP, writes=RP)
            A(P, lambda e: e.tensor_copy(out=kf[:], in_=posi[:]), reads=RP, writes=RP)
            A(P, lambda e: e.tensor_tensor(out=v[:], in0=v[:], in1=kf[:], op=ALU.subtract), reads=RP, writes=RP)
            A(P, lambda e: e.tensor_scalar(out=tt[:], in0=v[:], scalar1=0.5, scalar2=1.0, op0=ALU.is_gt, op1=ALU.mult), reads=RP, writes=RP)
            A(P, lambda e: e.tensor_tensor(out=v[:], in0=v[:], in1=tt[:], op=ALU.subtract), reads=RP, writes=RP)
            A(P, lambda e: e.tensor_scalar(out=tt[:], in0=v[:], scalar1=-0.5, scalar2=1.0, op0=ALU.is_lt, op1=ALU.mult), reads=RP, writes=RP)
            A(P, lambda e: e.tensor_tensor(out=v[:], in0=v[:], in1=tt[:], op=ALU.add), reads=RP, writes=RP)
            A(P, lambda e: e.tensor_scalar(out=uc[:], in0=v[:], scalar1=0.25, scalar2=None, op0=ALU.add), reads=RP, writes=RP)
            A(P, lambda e: e.tensor_scalar(out=tt[:], in0=uc[:], scalar1=0.5, scalar2=1.0, op0=ALU.is_gt, op1=ALU.mult), reads=RP, writes=RP)
            A(P, lambda e: e.tensor_tensor(out=uc[:], in0=uc[:], in1=tt[:], op=ALU.subtract), reads=RP, writes=RP)
            A("act", lambda e: e.activation(out=sinT[:], in_=v[:], func=AF.Sin, scale=TWO_PI), reads=RP, writes=[r_rope])
            A("act", lambda e: e.activation(out=cosT[:], in_=uc[:], func=AF.Sin, scale=TWO_PI), reads=RP, writes=[r_rope])

        def prologue_p2():
            for blk in range(NB):
                for half in range(2):
                    pt, r_pt = next_pt()
                    for k in range(8):
                        kc = half * 8 + k
                        A("pe", lambda e, pt=pt, k=k, kc=kc, blk=blk: e.transpose(
                            out=pt[:, k * 128:(k + 1) * 128], in_=hn[blk][:, kc * 128:(kc + 1) * 128], identity=ident[:]),
                          reads=[r_hn[blk]] + CONST, writes=[r_pt])
                    src = pt[:, :].rearrange("p (k c) -> p k c", k=8)
                    dst = hT[:, half * 8:half * 8 + 8, blk * 128:(blk + 1) * 128]
                    gb = gin[:, half * 8:half * 8 + 8].unsqueeze(2).to_broadcast([128, 8, 128])
                    A("dve", lambda e, src=src, dst=dst, gb=gb: e.tensor_tensor(out=dst, in0=src, in1=gb, op=ALU.mult),
                      reads=[r_pt] + CONST, writes=[r_hT])

        def proj_fm(bank, r_bank, w, r_w, wcol0, wstride, ncols_out=NT):
            for kc in range(KC):
                A("pe", lambda e, kc=kc: e.matmul(bank[:, 0:NT], lhsT=w[:, kc * wstride + wcol0: kc * wstride + wcol0 + 128],
                                                  rhs=hT[:, kc, :], start=(kc == 0), stop=(kc == KC - 1)),
                  reads=[r_w, r_hT], writes=[r_bank])

        pending = []

        def flush_pending():
            while pending:
                pending.pop(0)()

        def rope_evac(bank, r_bank, dst, r_dst):
            i = cnt.setdefault("rope", 0) % 2
            cnt["rope"] += 1
            A("act", lambda e: e.activation(out=qraw[i][:], in_=bank[:, 0:NT], func=AF.Copy),
              reads=[r_bank], writes=[r_qraw[i]])
            A("dve", lambda e: e.tensor_tensor(out=rt1[i][:], in0=bank[:, 0:NT], in1=cosT[:], op=ALU.mult),
              reads=[r_bank, r_rope], writes=[r_rt[i]])

            def second():
                b2, r_b2 = next_pb()
                A("pe", lambda e: e.matmul(b2[:, 0:NT], lhsT=prot[:], rhs=qraw[i][:], start=True, stop=True),
                  reads=[r_qraw[i]] + CONST, writes=[r_b2])
                A("dve", lambda e: e.tensor_tensor(out=rt2[i][:], in0=b2[:, 0:NT], in1=sinT[:], op=ALU.mult),
                  reads=[r_b2, r_rope, r_rt[i]], writes=[r_rt[i]])
                A(os.environ.get("DBG_ROPE_ENG", "pool"), lambda e: e.tensor_tensor(out=dst, in0=rt1[i][:], in1=rt2[i][:], op=ALU.add),
                  reads=[r_rt[i]], writes=[r_dst])
            pending.append(second)

        def attn_kv(ti):

            SUB = int(os.environ.get("DBG_SUB", "9"))
            wK, r_wK = get_w(G_K)
            for g in range(4):
                bank, r_bank = next_pb()
                proj_fm(bank, r_bank, wK, r_wK, g * 128, 512)
                flush_pending()
                if SUB == 1:
                    copy_op("act", KT[g][:, 128:128 + NT], bank[:, 0:NT], [r_bank], [r_KT[g]])
                    continue
                rope_evac(bank, r_bank, KT[g][:, 128:128 + NT], r_KT[g])
            if SUB <= 2:
                flush_pending()
                return
            wV, r_wV = get_w(G_V, KC * 256)
            for blk in range(NB):
                bank, r_bank = next_pb()
                for kc in range(KC):
                    A("pe", lambda e, kc=kc, blk=blk, bank=bank: e.matmul(
                        bank[:, 0:256], lhsT=hT[:, kc, blk * 128:(blk + 1) * 128], rhs=wV[:, kc * 256:(kc + 1) * 256],
                        start=(kc == 0), stop=(kc == KC - 1)), reads=[r_wV, r_hT], writes=[r_bank])
                flush_pending()
                copy_op(evac_engine(), Vt[:, blk + 1, :], bank[:, 0:256], [r_bank], [r_Vt])
            flush_pending()

        def attn_halo():
            for g in range(4):
                A("pool", lambda e, g=g: e.tensor_copy(out=KT[g][:, 0:128], in_=KT[g][:, NT:NT + 128]),
                  reads=[r_KT[g]], writes=[r_KT[g]])
            A("pool", lambda e: e.tensor_copy(out=Vt[:, 0, :], in_=Vt[:, NB, :]), reads=[r_Vt], writes=[r_Vt])

        def attn_proj(g):
            s = g % 2
            wQ, r_wQ = get_w(G_QA + g)
            for c in range(2):
                bank, r_bank = next_pb()
                proj_fm(bank, r_bank, wQ, r_wQ, c * 128, 512)
                flush_pending()
                rope_evac(bank, r_bank, QT[s][:, c, :], r_QT[s])
            for c in range(2):
                bank, r_bank = next_pb()
                proj_fm(bank, r_bank, wQ, r_wQ, 256 + c * 128, 512)
                flush_pending()
                A("act", lambda e, c=c, bank=bank: e.activation(out=AG[s][:, c, :], in_=bank[:, 0:NT], func=AF.Silu),
                  reads=[r_bank], writes=[r_AG[s]])
            flush_pending()

        def attn_A(g, ti):
            s = g % 2
            for j in range(NB):
                attn_stage_A(g, j, ti, s)

        def attn_CE(g):
            s = g % 2
            for j in range(NB):
                attn_stage_C(g, j)
            for j in range(NB):
                attn_stage_E(g, j, s)

        def attn_stage_A(g, j, ti, s):
            a = j % 2
            bE, r_bE = next_pb()
            bO, r_bO = next_pb()
            for i in range(4):
                c, half = i // 2, i % 2
                bank, r_bank = (bE, r_bE) if half == 0 else (bO, r_bO)
                lo = 64 * half
                A("pe", lambda e, c=c, lo=lo, bank=bank: e.matmul(
                    bank[:, c * 256:(c + 1) * 256], lhsT=QT[s][lo:lo + 64, c, j * 128:(j + 1) * 128],
                    rhs=KT[g][lo:lo + 64, j * 128:j * 128 + 256], start=True, stop=True),
                  reads=[r_QT[s], r_KT[g]], writes=[r_bank])
            mask = maskF if (ti == 0 and j == 0) else maskA
            stt = ast[a]
            A("dve", lambda e: e.tensor_tensor(out=Sm[a][:, 0:512], in0=bE[:, :], in1=mask[:], op=ALU.add),
              reads=[r_bE] + CONST, writes=[r_Sm[a]])
            A("dve", lambda e: e.tensor_tensor(out=Sm[a][:, 512:1024], in0=bO[:, :], in1=mask[:], op=ALU.add),
              reads=[r_bO] + CONST, writes=[r_Sm[a]])
            A("dve", lambda e: e.tensor_reduce(out=stt[:, 0:4], in_=Sm[a][:, :].rearrange("p (e k) -> p e k", e=4),
                                               axis=AX.X, op=ALU.max),
              reads=[r_Sm[a]], writes=[r_ast[a]])
            A("dve", lambda e: e.tensor_scalar(out=stt[:, 4:8], in0=stt[:, 0:4], scalar1=-0.125, scalar2=None, op0=ALU.mult),
              reads=[r_ast[a]], writes=[r_ast[a]])
            for ee in range(4):
                A("act", lambda e, ee=ee: e.activation(out=Pm[a][:, ee * 256:(ee + 1) * 256], in_=Sm[a][:, ee * 256:(ee + 1) * 256],
                                                       func=AF.Exp, scale=0.125, bias=stt[:, 4 + ee:5 + ee],
                                                       accum_out=stt[:, 8 + ee:9 + ee]),
                  reads=[r_Sm[a], r_ast[a]], writes=[r_Pm[a], r_ast[a]])
            A("act", lambda e: e.activation(out=stt[:, 12:16], in_=stt[:, 4:8], func=AF.Exp),
              reads=[r_ast[a]], writes=[r_ast[a]])
            A("dve", lambda e: e.tensor_tensor(out=stt[:, 12:16], in0=stt[:, 12:16], in1=esk[:, g * 4:g * 4 + 4], op=ALU.mult),
              reads=[r_ast[a]] + CONST, writes=[r_ast[a]])
            A("dve", lambda e: e.tensor_tensor(out=stt[:, 12:16], in0=stt[:, 12:16], in1=stt[:, 8:12], op=ALU.add),
              reads=[r_ast[a]], writes=[r_ast[a]])
            A("dve", lambda e: e.reciprocal(out=stt[:, 16:20], in_=stt[:, 12:16]),
              reads=[r_ast[a]], writes=[r_ast[a]])
            A("dve", lambda e: e.tensor_tensor(
                out=Pm[a][:, :].rearrange("p (e k) -> p e k", e=4), in0=Pm[a][:, :].rearrange("p (e k) -> p e k", e=4),
                in1=stt[:, 16:20].unsqueeze(2).to_broadcast([128, 4, 256]), op=ALU.mult),
              reads=[r_Pm[a], r_ast[a]], writes=[r_Pm[a]])

        def attn_stage_C(g, j):
            a = j % 2
            pt, r_pt = next_pt()
            for kb in range(2):
                for ee in range(4):
                    A("pe", lambda e, kb=kb, ee=ee: e.transpose(
                        out=pt[:, kb * 512 + ee * 128: kb * 512 + (ee + 1) * 128],
                        in_=Pm[a][:, ee * 256 + kb * 128: ee * 256 + (kb + 1) * 128], identity=ident[:]),
                      reads=[r_Pm[a]] + CONST, writes=[r_pt])
            copy_op(evac_engine(), PTs[a][:], pt[:, :], [r_pt], [r_PTs[a]])

        def attn_stage_E(g, j, s):
            a = j % 2
            bV, r_bV = next_pb()
            for half in range(2):
                for kb in range(2):
                    A("pe", lambda e, half=half, kb=kb: e.matmul(
                        bV[64 * half:64 * half + 64, 0:256], lhsT=Vt[:, j + kb, g * 64:(g + 1) * 64],
                        rhs=PTs[a][:, kb * 512 + half * 256: kb * 512 + half * 256 + 256],
                        start=(kb == 0), stop=(kb == 1)),
                      reads=[r_Vt, r_PTs[a]], writes=[r_bV])
            A("dve", lambda e: e.tensor_tensor(
                out=GA[:, 2 * g:2 * g + 2, j * 128:(j + 1) * 128],
                in0=bV[:, 0:256].rearrange("p (c t) -> p c t", c=2),
                in1=AG[s][:, :, j * 128:(j + 1) * 128], op=ALU.mult),
              reads=[r_bV, r_AG[s]], writes=[r_GA])

        def hgrn_prep1(hd, warm):
            s = hd % 2
            T1, T2, T3, T4, T5 = HT[s]
            R1, R2, R3, R4, R5 = r_HT[s]
            wH, r_wH = get_w(G_H + hd)
            bf, r_bf = next_pb()
            proj_fm(bf, r_bf, wH, r_wH, 0, 512)
            A("act", lambda e: e.activation(out=T1[:], in_=bf[:, 0:NT], func=AF.Sigmoid), reads=[r_bf], writes=[R1])
            A("dve", lambda e: e.tensor_scalar(out=T1[:], in0=T1[:], scalar1=oml[:, hd:hd + 1], scalar2=lb[:, hd:hd + 1],
                                               op0=ALU.mult, op1=ALU.add), reads=[R1] + CONST, writes=[R1])
            A("act", lambda e: e.activation(out=T2[:], in_=T1[:], func=AF.Ln), reads=[R1], writes=[R2])
            A("dve", lambda e: e.tensor_tensor_scan(out=T3[:], data0=rmask[:], data1=T2[:], initial=0.0,
                                                    op0=ALU.mult, op1=ALU.add), reads=[R2] + CONST, writes=[R3])
            A("pool", lambda e: e.tensor_scalar(out=T1[:], in0=T1[:], scalar1=-1.0, scalar2=1.0, op0=ALU.mult, op1=ALU.add),
              reads=[R1], writes=[R1])
            A("act", lambda e: e.activation(out=T2[:], in_=T3[:], func=AF.Exp), reads=[R3, R2], writes=[R2])
            A("act", lambda e: e.activation(out=T4[:], in_=T3[:], func=AF.Exp, scale=-1.0), reads=[R3], writes=[R4])
            A("dve", lambda e: e.tensor_copy(out=EB[hd][:], in_=T2[:, :].rearrange("p (c t) -> p c t", t=64)[:, :, 63]),
              reads=[R2], writes=[r_hd[hd]])
            A("dve", lambda e: e.tensor_tensor(out=KH[s][:], in0=T1[:], in1=T4[:], op=ALU.mult),
              reads=[R1, R4], writes=[r_KH[s]])
            bv, r_bv = next_pb()
            for blk in range(NB):
                for kc in range(KC):
                    A("pe", lambda e, kc=kc, blk=blk: e.matmul(
                        bv[:, blk * 128:(blk + 1) * 128], lhsT=hT[:, kc, blk * 128:(blk + 1) * 128],
                        rhs=wH[:, kc * 512 + 256: kc * 512 + 384], start=(kc == 0), stop=(kc == KC - 1)),
                      reads=[r_wH, r_hT], writes=[r_bv])
            A("act", lambda e: e.activation(out=VH[hd][:, :, :], in_=bv[:, 0:NB * 128].rearrange("p (b c) -> p b c", b=NB),
                                            func=AF.Copy), reads=[r_bv], writes=[r_hd[hd]])
            if not warm:
                bq, r_bq = next_pb()
                proj_fm(bq, r_bq, wH, r_wH, 128, 512)
                A("act", lambda e: e.activation(out=T5[:], in_=bq[:, 0:NT], func=AF.Silu), reads=[r_bq], writes=[R5])
                A("dve", lambda e: e.scalar_tensor_tensor(out=QH[hd][:], in0=T5[:], scalar=float(128 ** -0.5), in1=T2[:],
                                                          op0=ALU.mult, op1=ALU.mult), reads=[R5, R2], writes=[r_hd[hd]])
                bg, r_bg = next_pb()
                proj_fm(bg, r_bg, wH, r_wH, 384, 512)
                A("act", lambda e: e.activation(out=HGs[:, hd, :], in_=bg[:, 0:NT], func=AF.Silu),
                  reads=[r_bg], writes=[r_HGs[hd]])
        def hgrn_prep2(hd, warm):
            s = hd % 2
            pt, r_pt = next_pt()
            for blk in range(NB):
                A("pe", lambda e, blk=blk: e.transpose(out=pt[:, blk * 128:(blk + 1) * 128],
                                                       in_=KH[s][:, blk * 128:(blk + 1) * 128], identity=ident[:]),
                  reads=[r_KH[s]] + CONST, writes=[r_pt])
            A("dve", lambda e: e.tensor_copy(out=KTok[hd][:, :, :], in_=pt[:, 0:NB * 128].rearrange("p (b c) -> p b c", b=NB)),
              reads=[r_pt], writes=[r_hd[hd]])
            if not warm:
                bA, r_bA = next_pb()
                for blk in range(NB):
                    A("pe", lambda e, blk=blk: e.matmul(bA[:, blk * 128:(blk + 1) * 128], lhsT=KH[s][:, blk * 128:(blk + 1) * 128],
                                                        rhs=QH[hd][:, blk * 128:(blk + 1) * 128], start=True, stop=True),
                      reads=[r_KH[s], r_hd[hd]], writes=[r_bA])
                A("dve", lambda e: e.tensor_tensor(
                    out=ATm[hd][:, :].rearrange("p (b t) -> p b t", b=NB),
                    in0=bA[:, 0:NT].rearrange("p (b t) -> p b t", b=NB),
                    in1=maskH[:, :].unsqueeze(1).to_broadcast([128, NB, 128]), op=ALU.mult),
                  reads=[r_bA] + CONST, writes=[r_hd[hd]])

        def hgrn_core(warm):
            deferred = None
            for blk in range(NB):
                banks = hgrn_core_mm(blk, warm)
                if warm:
                    continue
                if deferred is not None:
                    deferred([banks[0][1], banks[1][1]])
                hgrn_evac(blk, banks)
                deferred = (lambda live, blk=blk: hgrn_out(blk, live))
            return deferred

        def hgrn_core_mm(blk, warm):
            banks = []
            if not warm:
                banks = [next_pb(), next_pb()]
                for hd in range(8):
                    b, r_b = banks[hd // 4]
                    q = hd % 4
                    A("pe", lambda e, hd=hd, b=b, q=q: e.matmul(
                        b[:, q * 128:(q + 1) * 128], lhsT=VH[hd][:, blk, :], rhs=ATm[hd][:, blk * 128:(blk + 1) * 128],
                        start=(q == 0), stop=False, skip_group_check=True),
                      reads=[r_hd[hd]], writes=[r_b])
                hgrn_inter(blk, 0, banks)
            sb_a = [next_pb(), next_pb()]
            hgrn_kv(blk, 0, sb_a)
            sb_b = [next_pb(), next_pb()]
            hgrn_kv(blk, 1, sb_b)
            hgrn_update(blk, 0, sb_a)
            if not warm:
                hgrn_inter(blk, 1, banks)
            hgrn_update(blk, 1, sb_b)
            return banks

        def hgrn_inter(blk, half, banks):
            lo = 64 * half
            for hd in range(8):
                b, r_b = banks[hd // 4]
                q = hd % 4
                A("pe", lambda e, hd=hd, b=b, q=q: e.matmul(
                    b[:, q * 128 + lo: q * 128 + lo + 64], lhsT=Sb[hd][half][:],
                    rhs=QH[hd][:, blk * 128 + lo: blk * 128 + lo + 64],
                    start=False, stop=(half == 1 and q == 3), skip_group_check=True),
                  reads=[r_hd[hd], r_Sb[hd][half]], writes=[r_b])

        def hgrn_kv(blk, half, sbanks):
            lo = 64 * half
            for hd in range(8):
                bs, r_bs = sbanks[hd // 4]
                q = hd % 4
                A("pe", lambda e, hd=hd, bs=bs, q=q: e.matmul(
                    bs[:, q * 128:(q + 1) * 128], lhsT=KTok[hd][lo:lo + 64, blk, :], rhs=VH[hd][lo:lo + 64, blk, :],
                    start=True, stop=True), reads=[r_hd[hd]], writes=[r_bs])

        def hgrn_update(blk, half, sbanks):
            c = blk * 2 + half
            for hd in range(8):
                bs, r_bs = sbanks[hd // 4]
                q = hd % 4
                A("dve", lambda e, hd=hd, bs=bs, q=q: e.tensor_tensor(
                    out=Tst[hd][:], in0=bs[:, q * 128:(q + 1) * 128], in1=Sf[hd][:], op=ALU.add),
                  reads=[r_bs, r_Sf[hd]], writes=[r_Tst[hd]])
                A("act", lambda e, hd=hd: e.activation(out=Sb[hd][1 - half][:], in_=Tst[hd][:], func=AF.Identity,
                                                      scale=EB[hd][:, c:c + 1]),
                  reads=[r_Tst[hd], r_hd[hd]], writes=[r_Sb[hd][1 - half]])
                A("dve", lambda e, hd=hd: e.tensor_scalar(out=Sf[hd][:], in0=Tst[hd][:], scalar1=EB[hd][:, c:c + 1],
                                                       scalar2=None, op0=ALU.mult),
                  reads=[r_Tst[hd], r_hd[hd]], writes=[r_Sf[hd]])

        def hgrn_evac(blk, banks):
            for bi in range(2):
                b, r_b = banks[bi]
                copy_op("act" if bi == 0 else "dve", OT[bi][:], b[:, :], [r_b], [r_OT[bi]])

        def hgrn_out(blk, live):
            for bi in range(2):
                hgrn_out_bank(blk, bi, live)

        def hgrn_out_bank(blk, bi, live):
            a = bi
            A("act", lambda e: e.activation(out=SQ[a][:], in_=OT[bi][:], func=AF.Square),
              reads=[r_OT[bi]], writes=[r_SQ[a]])
            bm, r_bm = next_pb(avoid=live)
            A("pe", lambda e: e.matmul(bm[:, :], lhsT=onesd[:], rhs=SQ[a][:], start=True, stop=True),
              reads=[r_SQ[a]] + CONST, writes=[r_bm])
            A("act", lambda e: e.activation(out=RR[a][:], in_=bm[:, :], func=AF.Ln, bias=EPS),
              reads=[r_bm], writes=[r_RR[a]])
            A("act", lambda e: e.activation(out=RR[a][:], in_=RR[a][:], func=AF.Exp, scale=-0.5),
              reads=[r_RR[a]], writes=[r_RR[a]])
            A("dve", lambda e: e.tensor_tensor(out=OT[bi][:], in0=OT[bi][:], in1=RR[a][:], op=ALU.mult),
              reads=[r_OT[bi], r_RR[a]], writes=[r_OT[bi]])
            for q in range(4):
                hd = bi * 4 + q
                A("dve", lambda e, hd=hd, q=q: e.scalar_tensor_tensor(
                    out=GH[:, hd, blk * 128:(blk + 1) * 128], in0=OT[bi][:, q * 128:(q + 1) * 128],
                    scalar=hgain[:, hd:hd + 1], in1=HGs[:, hd, blk * 128:(blk + 1) * 128],
                    op0=ALU.mult, op1=ALU.mult),
                  reads=[r_OT[bi], r_HGs[hd]] + CONST, writes=[r_GH])

        def merge_stage(deferred=None):
            for dc in range(16):
                merge_dc(dc, deferred if dc == 0 else None)

        def merge_dc(dc, deferred):
            a = dc % 2
            wM, r_wM = get_w(G_M + dc, 6144)
            bya, r_bya = next_pb()
            for kc in range(8):
                A("pe", lambda e, kc=kc: e.matmul(bya[:, 0:NT], lhsT=wM[:, 4096 + kc * 128: 4096 + (kc + 1) * 128],
                                                  rhs=GA[:, kc, :], start=(kc == 0), stop=(kc == 7)),
                  reads=[r_wM, r_GA], writes=[r_bya])
            bma, r_bma = next_pb()
            for kc in range(KC):
                A("pe", lambda e, kc=kc: e.matmul(bma[:, 0:NT], lhsT=wM[:, kc * 128:(kc + 1) * 128],
                                                  rhs=hT[:, kc, :], start=(kc == 0), stop=(kc == KC - 1)),
                  reads=[r_wM, r_hT], writes=[r_bma])
            bmh, r_bmh = next_pb()
            for kc in range(KC):
                A("pe", lambda e, kc=kc: e.matmul(bmh[:, 0:NT], lhsT=wM[:, 2048 + kc * 128: 2048 + (kc + 1) * 128],
                                                  rhs=hT[:, kc, :], start=(kc == 0), stop=(kc == KC - 1)),
                  reads=[r_wM, r_hT], writes=[r_bmh])
            m0, m1, m2, m3 = mt[a]
            q0, q1, q2, q3 = r_mt[a]
            A("act", lambda e: e.activation(out=m0[:], in_=bma[:, 0:NT], func=AF.Sigmoid), reads=[r_bma], writes=[q0])
            A("act", lambda e: e.activation(out=m1[:], in_=bmh[:, 0:NT], func=AF.Sigmoid), reads=[r_bmh], writes=[q1])
            A("dve", lambda e: e.tensor_tensor(out=m2[:], in0=bya[:, 0:NT], in1=m0[:], op=ALU.mult),
              reads=[r_bya, q0], writes=[q2])
            if deferred is not None:
                deferred([r_bya, r_bma, r_bmh])
            byh, r_byh = next_pb()
            for kc in range(8):
                A("pe", lambda e, kc=kc: e.matmul(byh[:, 0:NT], lhsT=wM[:, 5120 + kc * 128: 5120 + (kc + 1) * 128],
                                                  rhs=GH[:, kc, :], start=(kc == 0), stop=(kc == 7)),
                  reads=[r_wM, r_GH], writes=[r_byh])
            A("dve", lambda e: e.tensor_tensor(out=m3[:], in0=byh[:, 0:NT], in1=m1[:], op=ALU.mult),
              reads=[r_byh, q1], writes=[q3])
            A("pool", lambda e: e.tensor_tensor(out=MG[:, dc, :], in0=m2[:], in1=m3[:], op=ALU.add),
              reads=[q2, q3], writes=[r_MG])

        out_events = []

        def final_load(ti):
            row0 = (ti + 1) * NT
            for blk in range(NB):
                r0 = row0 + blk * 128
                A("sp", lambda e, blk=blk, r0=r0: e.dma_start(out=fin[blk][:], in_=x_d[r0:r0 + 128, :]),
                  writes=[r_fin[blk]], chan="fin%d" % blk)

        def final_stage(ti, first):
            for cg in range(4):
                final_cg(cg, first if cg == 0 else get_w(G_O + cg))

        def final_cg(cg, wpair):
            if True:
                wO, r_wO = wpair
                for blk in range(NB):
                    bank, r_bank = next_pb()
                    for kc in range(KC):
                        A("pe", lambda e, kc=kc, blk=blk, bank=bank: e.matmul(
                            bank[:, :], lhsT=MG[:, kc, blk * 128:(blk + 1) * 128], rhs=wO[:, kc * 512:(kc + 1) * 512],
                            start=(kc == 0), stop=(kc == KC - 1)), reads=[r_wO, r_MG], writes=[r_bank])
                    A("dve", lambda e, blk=blk, bank=bank, cg=cg: e.tensor_tensor(
                        out=fin[blk][:, cg * 512:(cg + 1) * 512], in0=bank[:, :], in1=fin[blk][:, cg * 512:(cg + 1) * 512],
                        op=ALU.add), reads=[r_bank, r_fin[blk]], writes=[r_fin[blk]])
                    A("act", lambda e, blk=blk, cg=cg: e.activation(out=junk[:], in_=fin[blk][:, cg * 512:(cg + 1) * 512],
                                                                    func=AF.Square, accum_out=fst[blk][:, 4 + cg:5 + cg]),
                      reads=[r_fin[blk]], writes=[r_junk, r_fst[blk]])

        def final_norm(ti):
            for blk in range(NB):
                f = fst[blk]
                A("dve", lambda e, f=f: e.tensor_reduce(out=f[:, 0:1], in_=f[:, 4:8], axis=AX.X, op=ALU.add),
                  reads=[r_fst[blk]], writes=[r_fst[blk]])
                A("act", lambda e, f=f: e.activation(out=f[:, 1:2], in_=f[:, 0:1], func=AF.Ln, scale=1.0 / D, bias=EPS),
                  reads=[r_fst[blk]], writes=[r_fst[blk]])
                A("act", lambda e, f=f: e.activation(out=f[:, 2:3], in_=f[:, 1:2], func=AF.Exp, scale=-0.5),
                  reads=[r_fst[blk]], writes=[r_fst[blk]])
                A("dve", lambda e, blk=blk, f=f: e.scalar_tensor_tensor(out=fin[blk][:], in0=fin[blk][:], scalar=f[:, 2:3],
                                                                      in1=fgain[:], op0=ALU.mult, op1=ALU.mult),
                  reads=[r_fin[blk], r_fst[blk]] + CONST, writes=[r_fin[blk]])
                r0 = ti * NT + blk * 128
                A("pool", lambda e, blk=blk, r0=r0: e.dma_start(out=out_d[r0:r0 + 128, :], in_=fin[blk][:]),
                  reads=[r_fin[blk]], writes=[r_fin[blk]], chan="fin%d" % blk)


        prologue_p1(-1)
        prologue_p2()
        for ti in range(-1, NTILES):
            warm = ti < 0
            attn_kv(ti)
            if not warm:
                attn_proj(0)
                attn_A(0, ti)
                for g in range(1, 4):
                    attn_proj(g)
                    attn_CE(g - 1)
                    attn_A(g, ti)
                hgrn_prep1(0, warm)
                attn_CE(3)
            else:
                hgrn_prep1(0, warm)
            attn_halo()
            for hd in range(1, 8):
                hgrn_prep1(hd, warm)
                hgrn_prep2(hd - 1, warm)
            hgrn_prep2(7, warm)
            deferred = hgrn_core(warm)
            if not warm:
                final_load(ti)
                merge_stage(deferred)
                wO_first = get_w(G_O + 0)
            if ti + 1 < NTILES:
                prologue_p1(ti + 1)
            if not warm:
                final_stage(ti, wO_first)
            if ti + 1 < NTILES:
                prologue_p2()
            if not warm:
                final_norm(ti)
        taps = []
        if os.environ.get("DBG_TAPS"):
            def tap(name, t, shape, dt, rl):
                d = nc.dram_tensor("tap_" + name, shape, dt, kind="ExternalOutput").ap()
                rr = Res("tap_" + name)
                A("sp", lambda e: e.dma_start(out=d, in_=t[:]), reads=rl, writes=[rr], chan="tap_" + name)
                taps.append(rr)
            tap("hT", hT, [128, KC, NT], BF16, [r_hT])
            for g in range(4):
                tap("KT%d" % g, KT[g], [128, 128 + NT], BF16, [r_KT[g]])
            tap("Vt", Vt, [128, NB + 1, 256], BF16, [r_Vt])
            tap("QT0", QT[0], [128, 2, NT], BF16, [r_QT[0]])
            tap("QT1", QT[1], [128, 2, NT], BF16, [r_QT[1]])
            tap("AG1", AG[1], [128, 2, NT], BF16, [r_AG[1]])
            tap("GA", GA, [128, 8, NT], BF16, [r_GA])
            tap("GH", GH, [128, 8, NT], BF16, [r_GH])
            tap("HGs", HGs, [128, 8, NT], BF16, r_HGs)
            tap("MG", MG, [128, KC, NT], BF16, [r_MG])
            tap("cosT", cosT, [128, NT], F32, [r_rope])
            tap("sinT", sinT, [128, NT], F32, [r_rope])
            tap("Sf0", Sf[0], [128, 128], F32, [r_Sf[0]])
            tap("ORAW", ORAW, [128, 512], F32, [r_ORAW])
            tap("OT", OT[0], [128, 512], F32, [r_OT[0]])
            tap("RR", RR[0], [128, 512], F32, [r_RR[0]])
            tap("Sb0", Sb[0][0], [128, 128], BF16, [r_Sb[0][0]])
            tap("QH0", QH[0], [128, NT], BF16, [r_hd[0]])
            tap("VH0", VH[0], [128, NB, 128], BF16, [r_hd[0]])
            tap("KTok0", KTok[0], [128, NB, 128], BF16, [r_hd[0]])
            tap("ATm0", ATm[0], [128, NT], BF16, [r_hd[0]])
            tap("EB0", EB[0], [128, NCH], F32, [r_hd[0]])
            A("sp", lambda e: e.nop(), reads=taps)
        A("pool", lambda e: e.nop(), reads=r_fin)

        S.prepare()
        with nc.Block() as block:
            @block.tensor
            def _(e):
                S.emit_one("pe", e)

            @block.scalar
            def _(e):
                S.emit_one("act", e)

            @block.vector
            def _(e):
                S.emit_one("dve", e)

            @block.gpsimd
            def _(e):
                S.emit_one("pool", e)

            @block.sync
            def _(e):
                S.emit_one("sp", e)
    return nc


def _consts():
    bf = ml_dtypes.bfloat16
    ident = np.eye(128, dtype=np.float32).astype(bf)
    onesd = np.full((128, 128), 1.0 / 128.0, dtype=np.float32).astype(bf)
    prot = np.zeros((128, 128), dtype=np.float32)
    for m in range(128):
        p = m + 32 if (m % 64) < 32 else m - 32
        prot[p, m] = 1.0
    prot = prot.astype(bf)
    q = np.arange(128)[:, None]
    k = np.arange(256)[None, :]
    rel = (q + 128) - k
    band = (rel >= 0) & (rel < 128)
    m1 = np.where(band, 0.0, -30000.0).astype(np.float32)
    maskA = np.concatenate([m1, m1], axis=1)
    mf = m1.copy()
    mf[:, :128] = -30000.0
    maskF0 = np.concatenate([mf, mf], axis=1)
    s = np.arange(128)[:, None]
    t = np.arange(128)[None, :]
    maskH = (((s // 64) == (t // 64)) & (s <= t)).astype(np.float32)
    rmask = np.ones((128, NT), dtype=np.float32)
    rmask[:, ::64] = 0.0
    half = 32
    inv_freq = (10000.0 ** (-np.arange(half, dtype=np.float32) / half)).astype(np.float32)
    p = np.arange(128)
    sgn = np.where((p % 64) < 32, -1.0, 1.0)
    invf = (sgn * inv_freq[p % 32].astype(np.float64) / (2.0 * np.pi)).astype(np.float32)[:, None]
    return dict(ident=ident, onesd=onesd, prot=prot, maskA=maskA, maskF0=maskF0, maskH=maskH, rmask=rmask, invf=invf)


_PROGRAM = None


def make_in_maps(inp, ncore=NCORE, seg=SEG):
    x = np.asarray(inp["x"], dtype=np.float32)
    positions = np.asarray(inp["positions"], dtype=np.int32)
    c = _consts()
    w_in0 = np.ascontiguousarray(np.asarray(inp["w_in"], dtype=np.float32)[0])
    w_ao0 = np.ascontiguousarray(np.asarray(inp["w_attn_out"], dtype=np.float32)[0])
    w_ho0 = np.ascontiguousarray(np.asarray(inp["w_hgrn_out"], dtype=np.float32)[0])
    w_o0 = np.ascontiguousarray(np.asarray(inp["w_o"], dtype=np.float32)[0])
    gin = np.ascontiguousarray(np.asarray(inp["norm_gain"], dtype=np.float32)[0].reshape(KC, 128).T)
    fgain = np.asarray(inp["final_norm_gain"], dtype=np.float32).reshape(1, D)
    sinks = np.asarray(inp["attn_sinks"], dtype=np.float32)[0]
    sink_perm = np.array([sinks[4 * g + EPERM[e]] for g in range(4) for e in range(4)], dtype=np.float32).reshape(1, 16)
    lbr = np.asarray(inp["hgrn_lower_bounds"], dtype=np.float32)
    lbraw = np.ascontiguousarray(lbr.reshape(2, 8, 128).transpose(2, 0, 1))
    hgain = np.ascontiguousarray(np.asarray(inp["hgrn_norm_gain"], dtype=np.float32)[0].T)
    nseg = x.shape[1] // seg
    in_maps = []
    for core in range(ncore):
        b, s = core // nseg, core % nseg
        t0 = s * seg
        xe = np.zeros((WARM + seg, D), dtype=np.float32)
        pe = np.zeros((1, WARM + seg), dtype=np.int32)
        xe[WARM:] = x[b, t0:t0 + seg]
        pe[0, WARM:] = positions[b, t0:t0 + seg]
        if s > 0:
            xe[:WARM] = x[b, t0 - WARM:t0]
            pe[0, :WARM] = positions[b, t0 - WARM:t0]
        in_maps.append({
            "x": xe, "pos": pe, "w_in": w_in0, "w_ao": w_ao0, "w_ho": w_ho0, "w_o": w_o0,
            "ident": c["ident"], "onesd": c["onesd"], "prot": c["prot"], "maskA": c["maskA"],
            "maskF": c["maskF0"] if s == 0 else c["maskA"], "maskH": c["maskH"], "rmask": c["rmask"],
            "invf": c["invf"], "gin": gin, "fgain": fgain, "sink": sink_perm, "lbraw": lbraw, "hgain": hgain,
        })
    return in_maps


def kernel(x, positions, norm_gain, w_in, attn_sinks, hgrn_lower_bounds, hgrn_norm_gain,
           w_attn_out, w_hgrn_out, w_o, final_norm_gain):
    global _PROGRAM
    in_maps = make_in_maps(dict(x=x, positions=positions, norm_gain=norm_gain, w_in=w_in, attn_sinks=attn_sinks,
                                hgrn_lower_bounds=hgrn_lower_bounds, hgrn_norm_gain=hgrn_norm_gain,
                                w_attn_out=w_attn_out, w_hgrn_out=w_hgrn_out, w_o=w_o,
                                final_norm_gain=final_norm_gain))
    if _PROGRAM is None:
        _PROGRAM = build_program()
    res = run_bass_kernel_spmd(_PROGRAM, in_maps, core_ids=list(range(NCORE)))
    out = np.empty((2, SEQ, D), dtype=np.float32)
    for core in range(NCORE):
        b, s = core // 4, core % 4
        out[b, s * SEG:(s + 1) * SEG] = np.asarray(res.results[core]["out"], dtype=np.float32)
    return out
```

```python
import os
import numpy as np
import ml_dtypes
from contextlib import ExitStack
import concourse.bass as bass
import concourse.mybir as mybir
from concourse.bass_utils import run_bass_kernel_spmd

F32 = mybir.dt.float32
BF16 = mybir.dt.bfloat16
I32 = mybir.dt.int32
AF = mybir.ActivationFunctionType
ALU = mybir.AluOpType
AX = mybir.AxisListType

SAME_ENG_SYNC = True

D = 2048
SEQ = 16384
NCORE = 8
SEG = 4096
NT = 256
NB = NT // 128
NCH = NT // 64
NTILES = SEG // NT
WARM = NT
KC = 16
EPS = 1e-6
TWO_PI = 6.2831845

OFF_AQ, OFF_AK, OFF_AV, OFF_AG = 0, 1024, 1280, 1536
OFF_HQ, OFF_HF, OFF_HI, OFF_HG = 2560, 3584, 4608, 5632
OFF_MA, OFF_MH = 6656, 8704
IN_W = 10752

G_K, G_V = 0, 1
G_QA = 2
G_H = 6
G_M = 14
G_O = 30
NGRP = 34
SLOT = 8192
NSLOT = 3
EPERM = (0, 2, 1, 3)


class Res:
    __slots__ = ("name", "writer", "readers", "const", "excl")

    def __init__(self, name, const=False, excl=False):
        self.name = name
        self.writer = None
        self.readers = []
        self.const = const
        self.excl = excl


class Op:
    __slots__ = ("eng", "fn", "deps", "marked", "event", "chan")

    def __init__(self, eng, fn, chan=None):
        self.eng = eng
        self.fn = fn
        self.deps = []
        self.marked = False
        self.event = None
        self.chan = chan


class Sched:
    ENG = ("pe", "act", "dve", "pool", "sp")

    def __init__(self, nc, stack):
        self.nc = nc
        self.stack = stack
        self.ops = {e: [] for e in self.ENG}
        self.sems = {}
        self.chan_cnt = {}
        self.n_wait = 0

    def sem(self, key):
        if key not in self.sems:
            name = "s_" + "_".join(str(k) for k in key)
            self.sems[key] = self.stack.enter_context(self.nc.semaphore(name))
        return self.sems[key]

    def add(self, eng, fn, reads=(), writes=(), chan=None):
        op = Op(eng, fn, chan)
        if any(r.excl for r in reads):
            writes = list(writes) + [r for r in reads if r.excl]
            reads = [r for r in reads if not r.excl]
        deps = {}
        for r in reads:
            if r.writer is not None:
                deps[id(r.writer)] = r.writer
        for w in writes:
            if w.writer is not None:
                deps[id(w.writer)] = w.writer
            for rd in w.readers:
                deps[id(rd)] = rd
        for r in reads:
            if not r.const:
                r.readers.append(op)
        for w in writes:
            w.writer = op
            w.readers = []
        dl = []
        for d in deps.values():
            if d is op:
                continue
            if d.eng == eng and d.chan is None and chan is None:
                if eng == "pe" or not SAME_ENG_SYNC:
                    continue
            dl.append(d)
        op.deps = dl
        if chan is not None:
            n = self.chan_cnt.get(chan, 0) + 16
            self.chan_cnt[chan] = n
            op.event = (("D", chan), n)
            op.marked = True
        self.ops[eng].append(op)
        return op

    def prepare(self):
        for e in self.ENG:
            for op in self.ops[e]:
                for d in op.deps:
                    d.marked = True
        for e in self.ENG:
            c = 0
            for op in self.ops[e]:
                if op.chan is None and op.marked:
                    c += 1
                    op.event = (("E", e), c)
        for e in self.ENG:
            for op in self.ops[e]:
                if op.event is not None:
                    self.sem(op.event[0])

    def emit_one(self, e, eng):
        seen = {}
        for op in self.ops[e]:
            waits = {}
            for d in op.deps:
                k, v = d.event
                if waits.get(k, 0) < v:
                    waits[k] = v
            for k, v in waits.items():
                if seen.get(k, 0) >= v:
                    continue
                seen[k] = v
                eng.wait_ge(self.sem(k), v)
                self.n_wait += 1
            inst = op.fn(eng)
            if op.chan is not None:
                inst.then_inc(self.sem(op.event[0]), 16)
            elif op.marked:
                inst.then_inc(self.sem(op.event[0]), 1)


def build_program(NTILES=NTILES):
    SEG = NTILES * NT
    nc = bass.Bass("TRN2", target_bir_lowering=False)

    def din(name, shape, dt):
        return nc.dram_tensor(name, shape, dt, kind="ExternalInput").ap()

    x_d = din("x", [WARM + SEG, D], F32)
    pos_d = din("pos", [1, WARM + SEG], I32)
    w_in_d = din("w_in", [D, IN_W], F32)
    w_ao_d = din("w_ao", [1024, D], F32)
    w_ho_d = din("w_ho", [1024, D], F32)
    w_o_d = din("w_o", [D, D], F32)
    ident_d = din("ident", [128, 128], BF16)
    onesd_d = din("onesd", [128, 128], BF16)
    prot_d = din("prot", [128, 128], BF16)
    maskA_d = din("maskA", [128, 512], F32)
    maskF_d = din("maskF", [128, 512], F32)
    maskH_d = din("maskH", [128, 128], F32)
    rmask_d = din("rmask", [128, NT], F32)
    invf_d = din("invf", [128, 1], F32)
    gin_d = din("gin", [128, KC], F32)
    fgain_d = din("fgain", [1, D], F32)
    sink_d = din("sink", [1, 16], F32)
    lbraw_d = din("lbraw", [128, 2, 8], F32)
    hgain_d = din("hgain", [128, 8], F32)
    out_d = nc.dram_tensor("out", [SEG, D], F32, kind="ExternalOutput").ap()
    wsc_d = nc.dram_tensor("wsc", [NGRP, 128, SLOT], BF16, kind="Internal").ap()

    with ExitStack() as st:
        S = Sched(nc, st)
        A = S.add

        def sb(name, shape, dt):
            return st.enter_context(nc.sbuf_tensor("sb_" + name, shape, dt))

        def ps(name, shape, dt):
            return st.enter_context(nc.psum_tensor("ps_" + name, shape, dt))

        ident = sb("ident", [128, 128], BF16)
        onesd = sb("onesd", [128, 128], BF16)
        prot = sb("prot", [128, 128], BF16)
        maskA = sb("maskA", [128, 512], F32)
        maskF = sb("maskF", [128, 512], F32)
        maskH = sb("maskH", [128, 128], F32)
        rmask = sb("rmask", [128, NT], F32)
        invf = sb("invf", [128, 1], F32)
        gin = sb("gin", [128, KC], F32)
        fgain = sb("fgain", [128, D], F32)
        sinkb = sb("sinkb", [128, 16], F32)
        esk = sb("esk", [128, 16], F32)
        lbraw = sb("lbraw", [128, 2, 8], F32)
        lbd = sb("lbd", [128, 8], F32)
        lb = sb("lb", [128, 8], F32)
        oml = sb("oml", [128, 8], F32)
        lnoml = sb("lnoml", [128, 8], F32)
        hgain = sb("hgain", [128, 8], F32)
        r_const = Res("const", const=True)
        cdma = [(ident, ident_d), (onesd, onesd_d), (prot, prot_d), (maskA, maskA_d), (maskF, maskF_d),
                (maskH, maskH_d), (rmask, rmask_d), (invf, invf_d), (gin, gin_d), (lbraw, lbraw_d),
                (hgain, hgain_d)]
        for t_, d_ in cdma:
            A("sp", lambda e, t_=t_, d_=d_: e.dma_start(out=t_[:], in_=d_), writes=[r_const], chan="const")
        A("sp", lambda e: e.dma_start(out=fgain[:], in_=fgain_d[0:1, :].to_broadcast([128, D])),
          writes=[r_const], chan="const")
        A("sp", lambda e: e.dma_start(out=sinkb[:], in_=sink_d[0:1, :].to_broadcast([128, 16])),
          writes=[r_const], chan="const")
        r_c2 = Res("const2")
        A("dve", lambda e: e.tensor_tensor(out=lbd[:], in0=lbraw[:, 0, :], in1=lbraw[:, 1, :], op=ALU.subtract),
          reads=[r_const], writes=[r_c2])
        A("act", lambda e: e.activation(out=lb[:], in_=lbd[:], func=AF.Sigmoid), reads=[r_c2], writes=[r_c2])
        A("act", lambda e: e.activation(out=oml[:], in_=lbd[:], func=AF.Sigmoid, scale=-1.0), reads=[r_c2], writes=[r_c2])
        A("act", lambda e: e.activation(out=lnoml[:], in_=oml[:], func=AF.Ln), reads=[r_c2], writes=[r_c2])
        A("act", lambda e: e.activation(out=esk[:], in_=sinkb[:], func=AF.Exp), reads=[r_const], writes=[r_c2])
        pass
        r_c2.const = True
        CONST = [r_const, r_c2]

        NPB = 6
        PB = [ps("pb%d" % i, [128, 512], F32) for i in range(NPB)]
        r_PB = [Res("pb%d" % i, excl=True) for i in range(NPB)]
        PT = [ps("pt%d" % i, [128, 1024], BF16) for i in range(2)]
        r_PT = [Res("pt%d" % i, excl=True) for i in range(2)]
        cnt = {"pb": 0, "pt": 0, "ws": 0, "ev": 0}

        def next_pb(avoid=()):
            while True:
                i = cnt["pb"] % NPB
                cnt["pb"] += 1
                if not any(r_PB[i] is r for r in avoid):
                    return PB[i], r_PB[i]

        def next_pt():
            i = cnt["pt"] % 2
            cnt["pt"] += 1
            return PT[i], r_PT[i]

        r_wsc = [Res("wsc%d" % g) for g in range(NGRP)]

        def cast(g, dst_lo, ncols, src, col0, nk):
            dst = wsc_d[g][:, dst_lo:dst_lo + nk * ncols].rearrange("p (k c) -> p k c", k=nk)
            s_ = src[:, col0:col0 + ncols].rearrange("(k p) c -> p k c", p=128)
            A("pool", lambda e: e.dma_start(out=dst, in_=s_), writes=[r_wsc[g]], chan="wsc%d" % g)

        def cast_sub(g, width, sub_lo, ncols, src, col0, nk):
            dst = wsc_d[g][:, 0:nk * width].rearrange("p (k c) -> p k c", k=nk)[:, :, sub_lo:sub_lo + ncols]
            s_ = src[:, col0:col0 + ncols].rearrange("(k p) c -> p k c", p=128)
            A("pool", lambda e: e.dma_start(out=dst, in_=s_), writes=[r_wsc[g]], chan="wsc%d" % g)

        def cast_group(g):
            if g == G_K:
                for kv in range(4):
                    for dup in range(2):
                        cast_sub(G_K, 512, kv * 128 + dup * 64, 64, w_in_d, OFF_AK + kv * 64, KC)
            elif g == G_V:
                cast_sub(G_V, 256, 0, 256, w_in_d, OFF_AV, KC)
            elif g < G_H:
                kv = g - G_QA
                cast_sub(g, 512, 0, 256, w_in_d, OFF_AQ + kv * 256, KC)
                cast_sub(g, 512, 256, 256, w_in_d, OFF_AG + kv * 256, KC)
            elif g < G_M:
                hd = g - G_H
                for q, off in enumerate((OFF_HF, OFF_HQ, OFF_HI, OFF_HG)):
                    cast_sub(g, 512, q * 128, 128, w_in_d, off + hd * 128, KC)
            elif g < G_O:
                dc = g - G_M
                cast(g, 0, 128, w_in_d, OFF_MA + dc * 128, KC)
                cast(g, 2048, 128, w_in_d, OFF_MH + dc * 128, KC)
                cast(g, 4096, 128, w_ao_d, dc * 128, 8)
                cast(g, 5120, 128, w_ho_d, dc * 128, 8)
            else:
                cg = g - G_O
                cast_sub(g, 512, 0, 512, w_o_d, cg * 512, KC)

        cast_order = [G_K, G_V] + [G_H + h for h in range(8)] + [G_QA + g for g in range(4)] \
            + [G_M + d for d in range(16)] + [G_O + c for c in range(4)]
        cast_done = set()
        CAST_AHEAD = 4

        def ensure_cast(g):
            pos = cast_order.index(g)
            for gg in cast_order[:pos + 1 + CAST_AHEAD]:
                if gg not in cast_done:
                    cast_done.add(gg)
                    cast_group(gg)

        WS = [sb("ws%d" % i, [128, SLOT], BF16) for i in range(NSLOT)]
        r_WS = [Res("ws%d" % i) for i in range(NSLOT)]

        def get_w(g, nelem=SLOT):
            ensure_cast(g)
            i = cnt["ws"] % NSLOT
            cnt["ws"] += 1
            A("sp", lambda e: e.dma_start(out=WS[i][:, 0:nelem], in_=wsc_d[g][:, 0:nelem]),
              reads=[r_wsc[g]], writes=[r_WS[i]], chan="ws%d" % i)
            return WS[i], r_WS[i]

        xs = sb("xs", [128, D], F32); r_xs = Res("xs")
        hn = [sb("hn%d" % i, [128, D], BF16) for i in range(NB)]
        r_hn = [Res("hn%d" % i) for i in range(NB)]
        st_small = [sb("stt%d" % i, [128, 4], F32) for i in range(NB)]
        r_sts = [Res("stt%d" % i) for i in range(NB)]
        hT = sb("hT", [128, KC, NT], BF16); r_hT = Res("hT")
        posi = sb("posi", [128, NT], I32); r_posi = Res("posi")
        HT = [[sb("HT0_%d" % k, [128, NT], F32) for k in range(5)]] * 2
        r_HT = [[Res("HT0_%d" % k) for k in range(5)]] * 2
        rp = HT[0][0:4]
        r_rp = r_HT[0][0:4]
        stmp = [sb("stmp%d" % i, [128, NT], F32) for i in range(2)]
        r_stmp = [Res("stmp%d" % i) for i in range(2)]
        cosT = sb("cosT", [128, NT], F32); sinT = sb("sinT", [128, NT], F32)
        r_rope = Res("rope")
        KT = [sb("KT%d" % g, [128, 128 + NT], BF16) for g in range(4)]
        r_KT = [Res("KT%d" % g) for g in range(4)]
        Vt = sb("Vt", [128, NB + 1, 256], BF16); r_Vt = Res("Vt")
        QT = [sb("QT%d" % i, [128, 2, NT], BF16) for i in range(2)]
        r_QT = [Res("QT%d" % i) for i in range(2)]
        AG = [sb("AG%d" % i, [128, 2, NT], BF16) for i in range(2)]
        r_AG = [Res("AG%d" % i) for i in range(2)]
        GA = sb("GA", [128, 8, NT], BF16); r_GA = Res("GA")
        GH = sb("GH", [128, 8, NT], BF16); r_GH = Res("GH")
        HGs = sb("HGs", [128, 8, NT], BF16); r_HGs = [Res("HGs%d" % h) for h in range(8)]
        MG = sb("MG", [128, KC, NT], BF16); r_MG = Res("MG")
        qraw = [sb("qraw%d" % i, [128, NT], BF16) for i in range(2)]
        r_qraw = [Res("qraw%d" % i) for i in range(2)]
        rt1 = [sb("rt1_%d" % i, [128, NT], F32) for i in range(2)]
        rt2 = [sb("rt2_%d" % i, [128, NT], F32) for i in range(2)]
        r_rt = [Res("rt%d" % i) for i in range(2)]
        Sm = [sb("Sm%d" % i, [128, 1024], F32) for i in range(2)]
        r_Sm = [Res("Sm%d" % i) for i in range(2)]
        Pm = [sb("Pm%d" % i, [128, 1024], BF16) for i in range(2)]
        r_Pm = [Res("Pm%d" % i) for i in range(2)]
        PTs = [sb("PTs%d" % i, [128, 1024], BF16) for i in range(2)]
        r_PTs = [Res("PTs%d" % i) for i in range(2)]
        ast = [sb("ast%d" % i, [128, 24], F32) for i in range(2)]
        r_ast = [Res("ast%d" % i) for i in range(2)]
        Sf = [sb("Sf%d" % h, [128, 128], F32) for h in range(8)]
        r_Sf = [Res("Sf%d" % h) for h in range(8)]
        Sb = [[sb("Sb%d_%d" % (h, p), [128, 128], BF16) for p in range(2)] for h in range(8)]
        r_Sb = [[Res("Sb%d_%d" % (h, p)) for p in range(2)] for h in range(8)]
        QH = [sb("QH%d" % h, [128, NT], BF16) for h in range(8)]
        VH = [sb("VH%d" % h, [128, NB, 128], BF16) for h in range(8)]
        KTok = [sb("KTok%d" % h, [128, NB, 128], BF16) for h in range(8)]
        ATm = [sb("ATm%d" % h, [128, NT], BF16) for h in range(8)]
        EB = [sb("EB%d" % h, [128, NCH], F32) for h in range(8)]
        r_hd = [Res("hd%d" % h) for h in range(8)]
        YQ = sb("YQ", [128, NT], F32); r_YQ = Res("YQ")
        KH = [sb("KH%d" % i, [128, NT], BF16) for i in range(2)]
        r_KH = [Res("KH%d" % i) for i in range(2)]
        Tst = [sb("Tst%d" % h, [128, 128], F32) for h in range(4)] * 2
        r_Tst = [Res("Tst%d" % h) for h in range(4)] * 2
        SQ = [sb("SQ0", [128, 512], BF16)] * 2
        r_SQ = [Res("SQ0")] * 2
        RR = [sb("RR0", [128, 512], F32)] * 2
        r_RR = [Res("RR0")] * 2
        OT = [sb("OT%d" % i, [128, 512], F32) for i in range(2)]
        r_OT = [Res("OT%d" % i) for i in range(2)]
        if os.environ.get("DBG_TAPS"):
            ORAW = sb("ORAW", [128, 512], F32); r_ORAW = Res("ORAW")
        mt = [HT[0][0:4]] * 2
        r_mt = [r_HT[0][0:4]] * 2
        fin = [sb("fin%d" % i, [128, D], F32) for i in range(NB)]
        r_fin = [Res("fin%d" % i) for i in range(NB)]
        junk = sb("junk", [128, 512], BF16); r_junk = Res("junk")
        fst = [sb("fst%d" % i, [128, 8], F32) for i in range(NB)]
        r_fst = [Res("fst%d" % i) for i in range(NB)]

        for h in range(8):
            A("pool", lambda e, h=h: e.memset(Sf[h][:], 0.0), writes=[r_Sf[h]])
            A("pool", lambda e, h=h: e.memset(Sb[h][0][:], 0.0), writes=[r_Sb[h][0]])
        for g in range(4):
            A("pool", lambda e, g=g: e.memset(KT[g][:], 0.0), writes=[r_KT[g]])
        A("pool", lambda e: e.memset(Vt[:], 0.0), writes=[r_Vt])

        def evac_engine():
            cnt["ev"] += 1
            return "act" if cnt["ev"] % 2 == 0 else "dve"

        def copy_op(eng, out, in_, reads, writes):
            if eng == "act":
                A("act", lambda e: e.activation(out=out, in_=in_, func=AF.Copy), reads=reads, writes=writes)
            else:
                A(eng, lambda e: e.tensor_copy(out=out, in_=in_), reads=reads, writes=writes)

        def act_silu_like(bank, r_bank, tmp, r_tmp):
            A("act", lambda e: e.activation(out=tmp[:], in_=bank[:, 0:NT], func=AF.Exp, scale=-1.0), reads=[r_bank], writes=[r_tmp])
            A("act", lambda e: e.activation(out=tmp[:], in_=tmp[:], func=AF.Ln, bias=1.0), reads=[r_tmp], writes=[r_tmp])
            A("act", lambda e: e.activation(out=tmp[:], in_=tmp[:], func=AF.Exp, scale=-1.0), reads=[r_tmp], writes=[r_tmp])

        def prologue_p1(ti):
            row0 = (ti + 1) * NT
            for blk in range(NB):
                r0 = row0 + blk * 128
                A("act", lambda e, r0=r0: e.dma_start(out=xs[:], in_=x_d[r0:r0 + 128, :]), writes=[r_xs], chan="xs")
                stt = st_small[blk]
                A("act", lambda e, blk=blk, stt=stt: e.activation(out=hn[blk][:], in_=xs[:], func=AF.Square,
                                                                accum_out=stt[:, 0:1]),
                  reads=[r_xs], writes=[r_hn[blk], r_sts[blk]])
                A("act", lambda e, stt=stt: e.activation(out=stt[:, 1:2], in_=stt[:, 0:1], func=AF.Ln,
                                                         scale=1.0 / D, bias=EPS),
                  reads=[r_sts[blk]], writes=[r_sts[blk]])
                A("act", lambda e, stt=stt: e.activation(out=stt[:, 2:3], in_=stt[:, 1:2], func=AF.Exp, scale=-0.5),
                  reads=[r_sts[blk]], writes=[r_sts[blk]])
                A("dve", lambda e, blk=blk, stt=stt: e.tensor_scalar(out=hn[blk][:], in0=xs[:], scalar1=stt[:, 2:3],
                                                                  scalar2=None, op0=ALU.mult),
                  reads=[r_xs, r_sts[blk]], writes=[r_hn[blk]])
            A("act", lambda e: e.dma_start(out=posi[:], in_=pos_d[0:1, row0:row0 + NT].to_broadcast([128, NT])),
              writes=[r_posi], chan="posi")
            v, kf, tt, uc = rp[0], rp[1], rp[2], rp[3]
            RP = [r_posi] + r_rp
            P = "dve"
            A(P, lambda e: e.tensor_copy(out=v[:], in_=posi[:]), reads=RP + [r_rope], writes=RP)
            A(P, lambda e: e.tensor_scalar(out=v[:], in0=v[:], scalar1=invf[:, 0:1], scalar2=1.0, op0=ALU.mult, op1=ALU.mult),
              reads=RP + CONST, writes=RP)
            A(P, lambda e: e.tensor_copy(out=posi[:], in_=v[:]), reads=RP, writes=RP)
            A(P, lambda e: e.tensor_copy(out=kf[:], in_=posi[:]), reads=RP, writes=RP)
            A(P, lambda e: e.tensor_tensor(out=v[:], in0=v[:], in1=kf[:], op=ALU.subtract), reads=RP, writes=RP)
            A(P, lambda e: e.tensor_scalar(out=tt[:], in0=v[:], scalar1=0.5, scalar2=1.0, op0=ALU.is_gt, op1=ALU.mult), reads=RP, writes=RP)
            A(P, lambda e: e.tensor_tensor(out=v[:], in0=v[:], in1=tt[:], op=ALU.subtract), reads=RP, writes=RP)
            A(P, lambda e: e.tensor_scalar(out=tt[:], in0=v[:], scalar1=-0.5, scalar2=1.0, op0=ALU.is_lt, op1=ALU.mult), reads=RP, writes=RP)
            A(P, lambda e: e.tensor_tensor(out=v[:], in0=v[:], in1=tt[:], op=ALU.add), reads=RP, writes=RP)
            A(P, lambda e: e.tensor_scalar(out=uc[:], in0=v[:], scalar1=0.25, scalar2=None, op0=ALU.add), reads=RP, writes=RP)
            A(P, lambda e: e.tensor_scalar(out=tt[:], in0=uc[:], scalar1=0.5, scalar2=1.0, op0=ALU.is_gt, op1=ALU.mult), reads=RP, writes=RP)
            A(P, lambda e: e.tensor_tensor(out=uc[:], in0=uc[:], in1=tt[:], op=ALU.subtract), reads=RP, writes=RP)
            A("act", lambda e: e.activation(out=sinT[:], in_=v[:], func=AF.Sin, scale=TWO_PI), reads=RP, writes=[r_rope])
            A("act", lambda e: e.activation(out=cosT[:], in_=uc[:], func=AF.Sin, scale=TWO_PI), reads=RP, writes=[r_rope])
            pass

        def prologue_p2():
            for blk in range(NB):
                for half in range(2):
                    pt, r_pt = next_pt()
                    for k in range(8):
                        kc = half * 8 + k
                        A("pe", lambda e, pt=pt, k=k, kc=kc, blk=blk: e.transpose(
                            out=pt[:, k * 128:(k + 1) * 128], in_=hn[blk][:, kc * 128:(kc + 1) * 128], identity=ident[:]),
                          reads=[r_hn[blk]] + CONST, writes=[r_pt])
                    src = pt[:, :].rearrange("p (k c) -> p k c", k=8)
                    dst = hT[:, half * 8:half * 8 + 8, blk * 128:(blk + 1) * 128]
                    gb = gin[:, half * 8:half * 8 + 8].unsqueeze(2).to_broadcast([128, 8, 128])
                    A("dve", lambda e, src=src, dst=dst, gb=gb: e.tensor_tensor(out=dst, in0=src, in1=gb, op=ALU.mult),
                      reads=[r_pt] + CONST, writes=[r_hT])

        def proj_fm(bank, r_bank, w, r_w, wcol0, wstride, ncols_out=NT):
            for kc in range(KC):
                A("pe", lambda e, kc=kc: e.matmul(bank[:, 0:NT], lhsT=w[:, kc * wstride + wcol0: kc * wstride + wcol0 + 128],
                                                  rhs=hT[:, kc, :], start=(kc == 0), stop=(kc == KC - 1)),
                  reads=[r_w, r_hT], writes=[r_bank])

        pending = []

        def flush_pending():
            while pending:
                pending.pop(0)()

        def rope_evac(bank, r_bank, dst, r_dst):
            i = cnt.setdefault("rope", 0) % 2
            cnt["rope"] += 1
            A("act", lambda e: e.activation(out=qraw[i][:], in_=bank[:, 0:NT], func=AF.Copy),
              reads=[r_bank], writes=[r_qraw[i]])
            A("dve", lambda e: e.tensor_tensor(out=rt1[i][:], in0=bank[:, 0:NT], in1=cosT[:], op=ALU.mult),
              reads=[r_bank, r_rope], writes=[r_rt[i]])

            def second():
                b2, r_b2 = next_pb()
                A("pe", lambda e: e.matmul(b2[:, 0:NT], lhsT=prot[:], rhs=qraw[i][:], start=True, stop=True),
                  reads=[r_qraw[i]] + CONST, writes=[r_b2])
                A("dve", lambda e: e.tensor_tensor(out=rt2[i][:], in0=b2[:, 0:NT], in1=sinT[:], op=ALU.mult),
                  reads=[r_b2, r_rope, r_rt[i]], writes=[r_rt[i]])
                A(os.environ.get("DBG_ROPE_ENG", "pool"), lambda e: e.tensor_tensor(out=dst, in0=rt1[i][:], in1=rt2[i][:], op=ALU.add),
                  reads=[r_rt[i]], writes=[r_dst])
            pending.append(second)

        def attn_kv(ti):

            SUB = int(os.environ.get("DBG_SUB", "9"))
            wK, r_wK = get_w(G_K)
            for g in range(4):
                bank, r_bank = next_pb()
                proj_fm(bank, r_bank, wK, r_wK, g * 128, 512)
                flush_pending()
                if SUB == 1:
                    copy_op("act", KT[g][:, 128:128 + NT], bank[:, 0:NT], [r_bank], [r_KT[g]])
                    continue
                rope_evac(bank, r_bank, KT[g][:, 128:128 + NT], r_KT[g])
            if SUB <= 2:
                flush_pending()
                return
            wV, r_wV = get_w(G_V, KC * 256)
            for blk in range(NB):
                bank, r_bank = next_pb()
                for kc in range(KC):
                    A("pe", lambda e, kc=kc, blk=blk, bank=bank: e.matmul(
                        bank[:, 0:256], lhsT=hT[:, kc, blk * 128:(blk + 1) * 128], rhs=wV[:, kc * 256:(kc + 1) * 256],
                        start=(kc == 0), stop=(kc == KC - 1)), reads=[r_wV, r_hT], writes=[r_bank])
                flush_pending()
                copy_op(evac_engine(), Vt[:, blk + 1, :], bank[:, 0:256], [r_bank], [r_Vt])
            flush_pending()

        def attn_halo():
            for g in range(4):
                A("pool", lambda e, g=g: e.tensor_copy(out=KT[g][:, 0:128], in_=KT[g][:, NT:NT + 128]),
                  reads=[r_KT[g]], writes=[r_KT[g]])
            A("pool", lambda e: e.tensor_copy(out=Vt[:, 0, :], in_=Vt[:, NB, :]), reads=[r_Vt], writes=[r_Vt])

        def attn_proj(g):
            s = g % 2
            wQ, r_wQ = get_w(G_QA + g)
            for c in range(2):
                bank, r_bank = next_pb()
                proj_fm(bank, r_bank, wQ, r_wQ, c * 128, 512)
                flush_pending()
                rope_evac(bank, r_bank, QT[s][:, c, :], r_QT[s])
            for c in range(2):
                bank, r_bank = next_pb()
                proj_fm(bank, r_bank, wQ, r_wQ, 256 + c * 128, 512)
                flush_pending()
                k_ = cnt.setdefault("stmp", 0) % 2
                cnt["stmp"] += 1
                act_silu_like(bank, r_bank, stmp[k_], r_stmp[k_])
                A("dve", lambda e, c=c, bank=bank, k_=k_: e.tensor_tensor(out=AG[s][:, c, :], in0=bank[:, 0:NT], in1=stmp[k_][:], op=ALU.mult),
                  reads=[r_bank, r_stmp[k_]], writes=[r_AG[s]])
            flush_pending()

        def attn_A(g, ti):
            s = g % 2
            for j in range(NB):
                attn_stage_A1(g, j, ti, s)
            for j in range(NB):
                attn_stage_A2a(g, j)
            for j in range(NB):
                attn_stage_A2b(g, j)
            for j in range(NB):
                attn_stage_A2c(g, j)

        def attn_CE(g):
            s = g % 2
            for j in range(NB):
                attn_stage_C(g, j)
            for j in range(NB):
                attn_stage_E(g, j, s)

        def attn_stage_A1(g, j, ti, s):
            a = j % 2
            bE, r_bE = next_pb()
            bO, r_bO = next_pb()
            for i in range(4):
                c, half = i // 2, i % 2
                bank, r_bank = (bE, r_bE) if half == 0 else (bO, r_bO)
                lo = 64 * half
                A("pe", lambda e, c=c, lo=lo, bank=bank: e.matmul(
                    bank[:, c * 256:(c + 1) * 256], lhsT=QT[s][lo:lo + 64, c, j * 128:(j + 1) * 128],
                    rhs=KT[g][lo:lo + 64, j * 128:j * 128 + 256], start=True, stop=True),
                  reads=[r_QT[s], r_KT[g]], writes=[r_bank])
            mask = maskF if (ti == 0 and j == 0) else maskA
            A("dve", lambda e: e.tensor_tensor(out=Sm[a][:, 0:512], in0=bE[:, :], in1=mask[:], op=ALU.add),
              reads=[r_bE] + CONST, writes=[r_Sm[a]])
            A("dve", lambda e: e.tensor_tensor(out=Sm[a][:, 512:1024], in0=bO[:, :], in1=mask[:], op=ALU.add),
              reads=[r_bO] + CONST, writes=[r_Sm[a]])

        def attn_stage_A2a(g, j):
            a = j % 2
            stt = ast[a]
            A("dve", lambda e: e.tensor_reduce(out=stt[:, 0:4], in_=Sm[a][:, :].rearrange("p (e k) -> p e k", e=4),
                                               axis=AX.X, op=ALU.max),
              reads=[r_Sm[a]], writes=[r_ast[a]])
            A("dve", lambda e: e.tensor_scalar(out=stt[:, 4:8], in0=stt[:, 0:4], scalar1=-0.125, scalar2=None, op0=ALU.mult),
              reads=[r_ast[a]], writes=[r_ast[a]])

        def attn_stage_A2b(g, j):
            a = j % 2
            stt = ast[a]
            for ee in range(4):
                A("act", lambda e, ee=ee: e.activation(out=Pm[a][:, ee * 256:(ee + 1) * 256], in_=Sm[a][:, ee * 256:(ee + 1) * 256],
                                                       func=AF.Exp, scale=0.125, bias=stt[:, 4 + ee:5 + ee],
                                                       accum_out=stt[:, 8 + ee:9 + ee]),
                  reads=[r_Sm[a], r_ast[a]], writes=[r_Pm[a], r_ast[a]])
            A("act", lambda e: e.activation(out=stt[:, 12:16], in_=stt[:, 4:8], func=AF.Exp),
              reads=[r_ast[a]], writes=[r_ast[a]])

        def attn_stage_A2c(g, j):
            a = j % 2
            stt = ast[a]
            A("dve", lambda e: e.tensor_tensor(out=stt[:, 12:16], in0=stt[:, 12:16], in1=esk[:, g * 4:g * 4 + 4], op=ALU.mult),
              reads=[r_ast[a]] + CONST, writes=[r_ast[a]])
            A("dve", lambda e: e.tensor_tensor(out=stt[:, 12:16], in0=stt[:, 12:16], in1=stt[:, 8:12], op=ALU.add),
              reads=[r_ast[a]], writes=[r_ast[a]])
            A("dve", lambda e: e.reciprocal(out=stt[:, 16:20], in_=stt[:, 12:16]),
              reads=[r_ast[a]], writes=[r_ast[a]])
            A("dve", lambda e: e.tensor_tensor(
                out=Pm[a][:, :].rearrange("p (e k) -> p e k", e=4), in0=Pm[a][:, :].rearrange("p (e k) -> p e k", e=4),
                in1=stt[:, 16:20].unsqueeze(2).to_broadcast([128, 4, 256]), op=ALU.mult),
              reads=[r_Pm[a], r_ast[a]], writes=[r_Pm[a]])

        def attn_stage_C(g, j):
            a = j % 2
            pt, r_pt = next_pt()
            for kb in range(2):
                for ee in range(4):
                    A("pe", lambda e, kb=kb, ee=ee: e.transpose(
                        out=pt[:, kb * 512 + ee * 128: kb * 512 + (ee + 1) * 128],
                        in_=Pm[a][:, ee * 256 + kb * 128: ee * 256 + (kb + 1) * 128], identity=ident[:]),
                      reads=[r_Pm[a]] + CONST, writes=[r_pt])
            copy_op(evac_engine(), PTs[a][:], pt[:, :], [r_pt], [r_PTs[a]])

        def attn_stage_E(g, j, s):
            a = j % 2
            bV, r_bV = next_pb()
            for half in range(2):
                for kb in range(2):
                    A("pe", lambda e, half=half, kb=kb: e.matmul(
                        bV[64 * half:64 * half + 64, 0:256], lhsT=Vt[:, j + kb, g * 64:(g + 1) * 64],
                        rhs=PTs[a][:, kb * 512 + half * 256: kb * 512 + half * 256 + 256],
                        start=(kb == 0), stop=(kb == 1)),
                      reads=[r_Vt, r_PTs[a]], writes=[r_bV])
            A("dve", lambda e: e.tensor_tensor(
                out=GA[:, 2 * g:2 * g + 2, j * 128:(j + 1) * 128],
                in0=bV[:, 0:256].rearrange("p (c t) -> p c t", c=2),
                in1=AG[s][:, :, j * 128:(j + 1) * 128], op=ALU.mult),
              reads=[r_bV, r_AG[s]], writes=[r_GA])

        def hgrn_prep1(hd, warm):
            s = hd % 2
            T1, T2, T3, T4, T5 = HT[s]
            R1, R2, R3, R4, R5 = r_HT[s]
            wH, r_wH = get_w(G_H + hd)
            bf, r_bf = next_pb()
            proj_fm(bf, r_bf, wH, r_wH, 0, 512)
            A("act", lambda e: e.activation(out=T1[:], in_=bf[:, 0:NT], func=AF.Exp, scale=-1.0), reads=[r_bf], writes=[R1])
            A("act", lambda e: e.activation(out=T2[:], in_=T1[:], func=AF.Ln, bias=1.0), reads=[R1], writes=[R2])
            A("act", lambda e: e.activation(out=T3[:], in_=T1[:], func=AF.Ln, scale=lb[:, hd:hd + 1], bias=1.0),
              reads=[R1] + CONST, writes=[R3])
            A("dve", lambda e: e.tensor_tensor(out=T3[:], in0=T3[:], in1=T2[:], op=ALU.subtract), reads=[R3, R2], writes=[R3])
            A("dve", lambda e: e.tensor_tensor_scan(out=T4[:], data0=rmask[:], data1=T3[:], initial=0.0,
                                                    op0=ALU.mult, op1=ALU.add), reads=[R3] + CONST, writes=[R4])
            A("act", lambda e: e.activation(out=T3[:], in_=T4[:], func=AF.Exp), reads=[R4, R3], writes=[R3])
            A("dve", lambda e: e.tensor_copy(out=EB[hd][:], in_=T3[:, :].rearrange("p (c t) -> p c t", t=64)[:, :, 63]),
              reads=[R3], writes=[r_hd[hd]])
            A("dve", lambda e: e.tensor_tensor(out=T2[:], in0=bf[:, 0:NT], in1=T2[:], op=ALU.add), reads=[r_bf, R2], writes=[R2])
            A("dve", lambda e: e.tensor_tensor(out=T2[:], in0=T2[:], in1=T4[:], op=ALU.add), reads=[R2, R4], writes=[R2])
            A("act", lambda e: e.activation(out=KH[s][:], in_=T2[:], func=AF.Exp, scale=-1.0, bias=lnoml[:, hd:hd + 1]),
              reads=[R2] + CONST, writes=[r_KH[s]])
            bv, r_bv = next_pb()
            for blk in range(NB):
                for kc in range(KC):
                    A("pe", lambda e, kc=kc, blk=blk: e.matmul(
                        bv[:, blk * 128:(blk + 1) * 128], lhsT=hT[:, kc, blk * 128:(blk + 1) * 128],
                        rhs=wH[:, kc * 512 + 256: kc * 512 + 384], start=(kc == 0), stop=(kc == KC - 1)),
                      reads=[r_wH, r_hT], writes=[r_bv])
            A("act", lambda e: e.activation(out=VH[hd][:, :, :], in_=bv[:, 0:NB * 128].rearrange("p (b c) -> p b c", b=NB),
                                            func=AF.Copy), reads=[r_bv], writes=[r_hd[hd]])
            if not warm:
                bq, r_bq = next_pb()
                proj_fm(bq, r_bq, wH, r_wH, 128, 512)
                act_silu_like(bq, r_bq, T5, R5)
                A("dve", lambda e: e.tensor_tensor(out=T5[:], in0=T5[:], in1=T3[:], op=ALU.mult), reads=[R5, R3], writes=[R5])
                A("dve", lambda e: e.scalar_tensor_tensor(out=QH[hd][:], in0=bq[:, 0:NT], scalar=float(128 ** -0.5), in1=T5[:],
                                                          op0=ALU.mult, op1=ALU.mult), reads=[r_bq, R5], writes=[r_hd[hd]])
                bg, r_bg = next_pb()
                proj_fm(bg, r_bg, wH, r_wH, 384, 512)
                act_silu_like(bg, r_bg, T1, R1)
                A("dve", lambda e: e.tensor_tensor(out=HGs[:, hd, :], in0=bg[:, 0:NT], in1=T1[:], op=ALU.mult),
                  reads=[r_bg, R1], writes=[r_HGs[hd]])
        def hgrn_prep2(hd, warm):
            s = hd % 2
            pt, r_pt = next_pt()
            for blk in range(NB):
                A("pe", lambda e, blk=blk: e.transpose(out=pt[:, blk * 128:(blk + 1) * 128],
                                                       in_=KH[s][:, blk * 128:(blk + 1) * 128], identity=ident[:]),
                  reads=[r_KH[s]] + CONST, writes=[r_pt])
            A("dve", lambda e: e.tensor_copy(out=KTok[hd][:, :, :], in_=pt[:, 0:NB * 128].rearrange("p (b c) -> p b c", b=NB)),
              reads=[r_pt], writes=[r_hd[hd]])
            if not warm:
                bA, r_bA = next_pb()
                for blk in range(NB):
                    A("pe", lambda e, blk=blk: e.matmul(bA[:, blk * 128:(blk + 1) * 128], lhsT=KH[s][:, blk * 128:(blk + 1) * 128],
                                                        rhs=QH[hd][:, blk * 128:(blk + 1) * 128], start=True, stop=True),
                      reads=[r_KH[s], r_hd[hd]], writes=[r_bA])
                A("dve", lambda e: e.tensor_tensor(
                    out=ATm[hd][:, :].rearrange("p (b t) -> p b t", b=NB),
                    in0=bA[:, 0:NT].rearrange("p (b t) -> p b t", b=NB),
                    in1=maskH[:, :].unsqueeze(1).to_broadcast([128, NB, 128]), op=ALU.mult),
                  reads=[r_bA] + CONST, writes=[r_hd[hd]])

        def hgrn_core(warm):
            deferred = None
            for blk in range(NB):
                banks = hgrn_core_mm(blk, warm)
                if warm:
                    continue
                if deferred is not None:
                    deferred([banks[0][1], banks[1][1]])
                hgrn_evac(blk, banks)
                deferred = (lambda live, blk=blk: hgrn_out(blk, live))
            return deferred

        def hgrn_core_mm(blk, warm):
            banks = []
            if not warm:
                banks = [next_pb(), next_pb()]
                for hd in range(8):
                    b, r_b = banks[hd // 4]
                    q = hd % 4
                    A("pe", lambda e, hd=hd, b=b, q=q: e.matmul(
                        b[:, q * 128:(q + 1) * 128], lhsT=VH[hd][:, blk, :], rhs=ATm[hd][:, blk * 128:(blk + 1) * 128],
                        start=(q == 0), stop=False, skip_group_check=True),
                      reads=[r_hd[hd]], writes=[r_b])
                hgrn_inter(blk, 0, banks)
            sb_a = [next_pb(), next_pb()]
            hgrn_kv(blk, 0, sb_a)
            sb_b = [next_pb(), next_pb()]
            hgrn_kv(blk, 1, sb_b)
            hgrn_update(blk, 0, sb_a)
            if not warm:
                hgrn_inter(blk, 1, banks)
            hgrn_update(blk, 1, sb_b)
            return banks

        def hgrn_inter(blk, half, banks):
            lo = 64 * half
            for hd in range(8):
                b, r_b = banks[hd // 4]
                q = hd % 4
                A("pe", lambda e, hd=hd, b=b, q=q: e.matmul(
                    b[:, q * 128 + lo: q * 128 + lo + 64], lhsT=Sb[hd][half][:],
                    rhs=QH[hd][:, blk * 128 + lo: blk * 128 + lo + 64],
                    start=False, stop=(half == 1 and q == 3), skip_group_check=True),
                  reads=[r_hd[hd], r_Sb[hd][half]], writes=[r_b])

        def hgrn_kv(blk, half, sbanks):
            lo = 64 * half
            for hd in range(8):
                bs, r_bs = sbanks[hd // 4]
                q = hd % 4
                A("pe", lambda e, hd=hd, bs=bs, q=q: e.matmul(
                    bs[:, q * 128:(q + 1) * 128], lhsT=KTok[hd][lo:lo + 64, blk, :], rhs=VH[hd][lo:lo + 64, blk, :],
                    start=True, stop=True), reads=[r_hd[hd]], writes=[r_bs])

        def hgrn_update(blk, half, sbanks):
            c = blk * 2 + half
            for hd in range(8):
                bs, r_bs = sbanks[hd // 4]
                q = hd % 4
                A("dve", lambda e, hd=hd, bs=bs, q=q: e.tensor_tensor(
                    out=Tst[hd][:], in0=bs[:, q * 128:(q + 1) * 128], in1=Sf[hd][:], op=ALU.add),
                  reads=[r_bs, r_Sf[hd]], writes=[r_Tst[hd]])
                A("act", lambda e, hd=hd: e.activation(out=Sb[hd][1 - half][:], in_=Tst[hd][:], func=AF.Identity,
                                                      scale=EB[hd][:, c:c + 1]),
                  reads=[r_Tst[hd], r_hd[hd]], writes=[r_Sb[hd][1 - half]])
                A("dve", lambda e, hd=hd: e.tensor_scalar(out=Sf[hd][:], in0=Tst[hd][:], scalar1=EB[hd][:, c:c + 1],
                                                       scalar2=None, op0=ALU.mult),
                  reads=[r_Tst[hd], r_hd[hd]], writes=[r_Sf[hd]])

        def hgrn_evac(blk, banks):
            for bi in range(2):
                b, r_b = banks[bi]
                copy_op("act" if bi == 0 else "dve", OT[bi][:], b[:, :], [r_b], [r_OT[bi]])

        def hgrn_out(blk, live):
            for bi in range(2):
                hgrn_out_bank(blk, bi, live)

        def hgrn_out_bank(blk, bi, live):
            a = bi
            A("act", lambda e: e.activation(out=SQ[a][:], in_=OT[bi][:], func=AF.Square),
              reads=[r_OT[bi]], writes=[r_SQ[a]])
            bm, r_bm = next_pb(avoid=live)
            A("pe", lambda e: e.matmul(bm[:, :], lhsT=onesd[:], rhs=SQ[a][:], start=True, stop=True),
              reads=[r_SQ[a]] + CONST, writes=[r_bm])
            A("act", lambda e: e.activation(out=RR[a][:], in_=bm[:, :], func=AF.Ln, bias=EPS),
              reads=[r_bm], writes=[r_RR[a]])
            A("act", lambda e: e.activation(out=RR[a][:], in_=RR[a][:], func=AF.Exp, scale=-0.5),
              reads=[r_RR[a]], writes=[r_RR[a]])
            A("dve", lambda e: e.tensor_tensor(out=OT[bi][:], in0=OT[bi][:], in1=RR[a][:], op=ALU.mult),
              reads=[r_OT[bi], r_RR[a]], writes=[r_OT[bi]])
            for q in range(4):
                hd = bi * 4 + q
                A("dve", lambda e, hd=hd, q=q: e.scalar_tensor_tensor(
                    out=GH[:, hd, blk * 128:(blk + 1) * 128], in0=OT[bi][:, q * 128:(q + 1) * 128],
                    scalar=hgain[:, hd:hd + 1], in1=HGs[:, hd, blk * 128:(blk + 1) * 128],
                    op0=ALU.mult, op1=ALU.mult),
                  reads=[r_OT[bi], r_HGs[hd]] + CONST, writes=[r_GH])

        def merge_stage(deferred=None):
            for dc in range(16):
                merge_dc(dc, deferred if dc == 0 else None)

        def merge_dc(dc, deferred):
            a = dc % 2
            wM, r_wM = get_w(G_M + dc, 6144)
            bya, r_bya = next_pb()
            for kc in range(8):
                A("pe", lambda e, kc=kc: e.matmul(bya[:, 0:NT], lhsT=wM[:, 4096 + kc * 128: 4096 + (kc + 1) * 128],
                                                  rhs=GA[:, kc, :], start=(kc == 0), stop=(kc == 7)),
                  reads=[r_wM, r_GA], writes=[r_bya])
            bma, r_bma = next_pb()
            for kc in range(KC):
                A("pe", lambda e, kc=kc: e.matmul(bma[:, 0:NT], lhsT=wM[:, kc * 128:(kc + 1) * 128],
                                                  rhs=hT[:, kc, :], start=(kc == 0), stop=(kc == KC - 1)),
                  reads=[r_wM, r_hT], writes=[r_bma])
            bmh, r_bmh = next_pb()
            for kc in range(KC):
                A("pe", lambda e, kc=kc: e.matmul(bmh[:, 0:NT], lhsT=wM[:, 2048 + kc * 128: 2048 + (kc + 1) * 128],
                                                  rhs=hT[:, kc, :], start=(kc == 0), stop=(kc == KC - 1)),
                  reads=[r_wM, r_hT], writes=[r_bmh])
            m0, m1, m2, m3 = mt[a]
            q0, q1, q2, q3 = r_mt[a]
            act_silu_like(bma, r_bma, m0, q0)
            act_silu_like(bmh, r_bmh, m1, q1)
            A("dve", lambda e: e.tensor_tensor(out=m2[:], in0=bya[:, 0:NT], in1=m0[:], op=ALU.mult),
              reads=[r_bya, q0], writes=[q2])
            if deferred is not None:
                deferred([r_bya, r_bma, r_bmh])
            byh, r_byh = next_pb()
            for kc in range(8):
                A("pe", lambda e, kc=kc: e.matmul(byh[:, 0:NT], lhsT=wM[:, 5120 + kc * 128: 5120 + (kc + 1) * 128],
                                                  rhs=GH[:, kc, :], start=(kc == 0), stop=(kc == 7)),
                  reads=[r_wM, r_GH], writes=[r_byh])
            A("dve", lambda e: e.tensor_tensor(out=m3[:], in0=byh[:, 0:NT], in1=m1[:], op=ALU.mult),
              reads=[r_byh, q1], writes=[q3])
            A("pool", lambda e: e.tensor_tensor(out=MG[:, dc, :], in0=m2[:], in1=m3[:], op=ALU.add),
              reads=[q2, q3], writes=[r_MG])

        out_events = []

        def final_load(ti):
            row0 = (ti + 1) * NT
            for blk in range(NB):
                r0 = row0 + blk * 128
                A("sp", lambda e, blk=blk, r0=r0: e.dma_start(out=fin[blk][:], in_=x_d[r0:r0 + 128, :]),
                  writes=[r_fin[blk]], chan="fin%d" % blk)

        def final_stage(ti, first):
            for cg in range(4):
                final_cg(cg, first if cg == 0 else get_w(G_O + cg))

        def final_cg(cg, wpair):
            if True:
                wO, r_wO = wpair
                for blk in range(NB):
                    bank, r_bank = next_pb()
                    for kc in range(KC):
                        A("pe", lambda e, kc=kc, blk=blk, bank=bank: e.matmul(
                            bank[:, :], lhsT=MG[:, kc, blk * 128:(blk + 1) * 128], rhs=wO[:, kc * 512:(kc + 1) * 512],
                            start=(kc == 0), stop=(kc == KC - 1)), reads=[r_wO, r_MG], writes=[r_bank])
                    A("dve", lambda e, blk=blk, bank=bank, cg=cg: e.tensor_tensor(
                        out=fin[blk][:, cg * 512:(cg + 1) * 512], in0=bank[:, :], in1=fin[blk][:, cg * 512:(cg + 1) * 512],
                        op=ALU.add), reads=[r_bank, r_fin[blk]], writes=[r_fin[blk]])
                    A("act", lambda e, blk=blk, cg=cg: e.activation(out=junk[:], in_=fin[blk][:, cg * 512:(cg + 1) * 512],
                                                                    func=AF.Square, accum_out=fst[blk][:, 4 + cg:5 + cg]),
                      reads=[r_fin[blk]], writes=[r_junk, r_fst[blk]])

        def final_norm(ti):
            for blk in range(NB):
                f = fst[blk]
                A("dve", lambda e, f=f: e.tensor_reduce(out=f[:, 0:1], in_=f[:, 4:8], axis=AX.X, op=ALU.add),
                  reads=[r_fst[blk]], writes=[r_fst[blk]])
                A("act", lambda e, f=f: e.activation(out=f[:, 1:2], in_=f[:, 0:1], func=AF.Ln, scale=1.0 / D, bias=EPS),
                  reads=[r_fst[blk]], writes=[r_fst[blk]])
                A("act", lambda e, f=f: e.activation(out=f[:, 2:3], in_=f[:, 1:2], func=AF.Exp, scale=-0.5),
                  reads=[r_fst[blk]], writes=[r_fst[blk]])
                A("dve", lambda e, blk=blk, f=f: e.scalar_tensor_tensor(out=fin[blk][:], in0=fin[blk][:], scalar=f[:, 2:3],
                                                                      in1=fgain[:], op0=ALU.mult, op1=ALU.mult),
                  reads=[r_fin[blk], r_fst[blk]] + CONST, writes=[r_fin[blk]])
                r0 = ti * NT + blk * 128
                A("pool", lambda e, blk=blk, r0=r0: e.dma_start(out=out_d[r0:r0 + 128, :], in_=fin[blk][:]),
                  reads=[r_fin[blk]], writes=[r_fin[blk]], chan="fin%d" % blk)


        prologue_p1(-1)
        prologue_p2()
        for ti in range(-1, NTILES):
            warm = ti < 0
            attn_kv(ti)
            if not warm:
                attn_proj(0)
                attn_A(0, ti)
                for g in range(1, 4):
                    attn_proj(g)
                    attn_CE(g - 1)
                    attn_A(g, ti)
                hgrn_prep1(0, warm)
                hgrn_prep1(1, warm)
                attn_CE(3)
            else:
                hgrn_prep1(0, warm)
                hgrn_prep1(1, warm)
            attn_halo()
            hgrn_prep2(0, warm)
            for hd in range(2, 8):
                hgrn_prep1(hd, warm)
                hgrn_prep2(hd - 1, warm)
            hgrn_prep2(7, warm)
            deferred = hgrn_core(warm)
            if not warm:
                final_load(ti)
                merge_stage(deferred)
                wO_first = get_w(G_O + 0)
            if ti + 1 < NTILES:
                prologue_p1(ti + 1)
            if not warm:
                final_stage(ti, wO_first)
            if ti + 1 < NTILES:
                prologue_p2()
            if not warm:
                final_norm(ti)
        taps = []
        if os.environ.get("DBG_TAPS"):
            def tap(name, t, shape, dt, rl):
                d = nc.dram_tensor("tap_" + name, shape, dt, kind="ExternalOutput").ap()
                rr = Res("tap_" + name)
                A("sp", lambda e: e.dma_start(out=d, in_=t[:]), reads=rl, writes=[rr], chan="tap_" + name)
                taps.append(rr)
            tap("hT", hT, [128, KC, NT], BF16, [r_hT])
            for g in range(4):
                tap("KT%d" % g, KT[g], [128, 128 + NT], BF16, [r_KT[g]])
            tap("Vt", Vt, [128, NB + 1, 256], BF16, [r_Vt])
            tap("QT0", QT[0], [128, 2, NT], BF16, [r_QT[0]])
            tap("QT1", QT[1], [128, 2, NT], BF16, [r_QT[1]])
            tap("AG1", AG[1], [128, 2, NT], BF16, [r_AG[1]])
            tap("GA", GA, [128, 8, NT], BF16, [r_GA])
            tap("GH", GH, [128, 8, NT], BF16, [r_GH])
            tap("HGs", HGs, [128, 8, NT], BF16, r_HGs)
            tap("MG", MG, [128, KC, NT], BF16, [r_MG])
            tap("cosT", cosT, [128, NT], F32, [r_rope])
            tap("sinT", sinT, [128, NT], F32, [r_rope])
            tap("Sf0", Sf[0], [128, 128], F32, [r_Sf[0]])
            tap("ORAW", ORAW, [128, 512], F32, [r_ORAW])
            tap("OT", OT[0], [128, 512], F32, [r_OT[0]])
            tap("RR", RR[0], [128, 512], F32, [r_RR[0]])
            tap("Sb0", Sb[0][0], [128, 128], BF16, [r_Sb[0][0]])
            tap("QH0", QH[0], [128, NT], BF16, [r_hd[0]])
            tap("VH0", VH[0], [128, NB, 128], BF16, [r_hd[0]])
            tap("KTok0", KTok[0], [128, NB, 128], BF16, [r_hd[0]])
            tap("ATm0", ATm[0], [128, NT], BF16, [r_hd[0]])
            tap("EB0", EB[0], [128, NCH], F32, [r_hd[0]])
            A("sp", lambda e: e.nop(), reads=taps)
        A("pool", lambda e: e.nop(), reads=r_fin)

        S.prepare()
        with nc.Block() as block:
            @block.tensor
            def _(e):
                S.emit_one("pe", e)

            @block.scalar
            def _(e):
                S.emit_one("act", e)

            @block.vector
            def _(e):
                S.emit_one("dve", e)

            @block.gpsimd
            def _(e):
                S.emit_one("pool", e)

            @block.sync
            def _(e):
                S.emit_one("sp", e)
    return nc


def _consts():
    bf = ml_dtypes.bfloat16
    ident = np.eye(128, dtype=np.float32).astype(bf)
    onesd = np.full((128, 128), 1.0 / 128.0, dtype=np.float32).astype(bf)
    prot = np.zeros((128, 128), dtype=np.float32)
    for m in range(128):
        p = m + 32 if (m % 64) < 32 else m - 32
        prot[p, m] = 1.0
    prot = prot.astype(bf)
    q = np.arange(128)[:, None]
    k = np.arange(256)[None, :]
    rel = (q + 128) - k
    band = (rel >= 0) & (rel < 128)
    m1 = np.where(band, 0.0, -30000.0).astype(np.float32)
    maskA = np.concatenate([m1, m1], axis=1)
    mf = m1.copy()
    mf[:, :128] = -30000.0
    maskF0 = np.concatenate([mf, mf], axis=1)
    s = np.arange(128)[:, None]
    t = np.arange(128)[None, :]
    maskH = (((s // 64) == (t // 64)) & (s <= t)).astype(np.float32)
    rmask = np.ones((128, NT), dtype=np.float32)
    rmask[:, ::64] = 0.0
    half = 32
    inv_freq = (10000.0 ** (-np.arange(half, dtype=np.float32) / half)).astype(np.float32)
    p = np.arange(128)
    sgn = np.where((p % 64) < 32, -1.0, 1.0)
    invf = (sgn * inv_freq[p % 32].astype(np.float64) / (2.0 * np.pi)).astype(np.float32)[:, None]
    return dict(ident=ident, onesd=onesd, prot=prot, maskA=maskA, maskF0=maskF0, maskH=maskH, rmask=rmask, invf=invf)


_PROGRAM = None


def make_in_maps(inp, ncore=NCORE, seg=SEG):
    x = np.asarray(inp["x"], dtype=np.float32)
    positions = np.asarray(inp["positions"], dtype=np.int32)
    c = _consts()
    w_in0 = np.ascontiguousarray(np.asarray(inp["w_in"], dtype=np.float32)[0])
    w_ao0 = np.ascontiguousarray(np.asarray(inp["w_attn_out"], dtype=np.float32)[0])
    w_ho0 = np.ascontiguousarray(np.asarray(inp["w_hgrn_out"], dtype=np.float32)[0])
    w_o0 = np.ascontiguousarray(np.asarray(inp["w_o"], dtype=np.float32)[0])
    gin = np.ascontiguousarray(np.asarray(inp["norm_gain"], dtype=np.float32)[0].reshape(KC, 128).T)
    fgain = np.asarray(inp["final_norm_gain"], dtype=np.float32).reshape(1, D)
    sinks = np.asarray(inp["attn_sinks"], dtype=np.float32)[0]
    sink_perm = np.array([sinks[4 * g + EPERM[e]] for g in range(4) for e in range(4)], dtype=np.float32).reshape(1, 16)
    lbr = np.asarray(inp["hgrn_lower_bounds"], dtype=np.float32)
    lbraw = np.ascontiguousarray(lbr.reshape(2, 8, 128).transpose(2, 0, 1))
    hgain = np.ascontiguousarray(np.asarray(inp["hgrn_norm_gain"], dtype=np.float32)[0].T)
    nseg = x.shape[1] // seg
    in_maps = []
    for core in range(ncore):
        b, s = core // nseg, core % nseg
        t0 = s * seg
        xe = np.zeros((WARM + seg, D), dtype=np.float32)
        pe = np.zeros((1, WARM + seg), dtype=np.int32)
        xe[WARM:] = x[b, t0:t0 + seg]
        pe[0, WARM:] = positions[b, t0:t0 + seg]
        if s > 0:
            xe[:WARM] = x[b, t0 - WARM:t0]
            pe[0, :WARM] = positions[b, t0 - WARM:t0]
        in_maps.append({
            "x": xe, "pos": pe, "w_in": w_in0, "w_ao": w_ao0, "w_ho": w_ho0, "w_o": w_o0,
            "ident": c["ident"], "onesd": c["onesd"], "prot": c["prot"], "maskA": c["maskA"],
            "maskF": c["maskF0"] if s == 0 else c["maskA"], "maskH": c["maskH"], "rmask": c["rmask"],
            "invf": c["invf"], "gin": gin, "fgain": fgain, "sink": sink_perm, "lbraw": lbraw, "hgain": hgain,
        })
    return in_maps


def kernel(x, positions, norm_gain, w_in, attn_sinks, hgrn_lower_bounds, hgrn_norm_gain,
           w_attn_out, w_hgrn_out, w_o, final_norm_gain):
    global _PROGRAM
    in_maps = make_in_maps(dict(x=x, positions=positions, norm_gain=norm_gain, w_in=w_in, attn_sinks=attn_sinks,
                                hgrn_lower_bounds=hgrn_lower_bounds, hgrn_norm_gain=hgrn_norm_gain,
                                w_attn_out=w_attn_out, w_hgrn_out=w_hgrn_out, w_o=w_o,
                                final_norm_gain=final_norm_gain))
    if _PROGRAM is None:
        _PROGRAM = build_program()
    res = run_bass_kernel_spmd(_PROGRAM, in_maps, core_ids=list(range(NCORE)))
    out = np.empty((2, SEQ, D), dtype=np.float32)
    for core in range(NCORE):
        b, s = core // 4, core % 4
        out[b, s * SEG:(s + 1) * SEG] = np.asarray(res.results[core]["out"], dtype=np.float32)
    return out
```
